# Optimizing a Trainium2 kernel written in Bass

```python
import jax, jax.numpy as jnp
from jax import lax
import numpy as np


D_MODEL = 4096
BATCH = 4
SEQ = 4096
DEPTH = 1

PLE_DIM = 256
ATT_HEAD_DIM = 128
ATT_HEADS_PER_GROUP = 8
ATT_GROUPS = ((128, 1), (512, 4), (2048, 16))
N_ATT_GROUPS = 3
ATT_BLOCK = 128
ATT_QKV_WIDTH = N_ATT_GROUPS * ATT_HEADS_PER_GROUP * ATT_HEAD_DIM
ATT_OUT_WIDTH = ATT_HEADS_PER_GROUP * ATT_HEAD_DIM
RET_HEADS = 8
RET_QK_DIM = 256
RET_V_DIM = 512
RET_QK_WIDTH = RET_HEADS * RET_QK_DIM
RET_V_WIDTH = RET_HEADS * RET_V_DIM
RET_CHUNK = 128
RET_ROPE_BASE = 10000.0
IN_WIDTH = 3 * ATT_QKV_WIDTH + 2 * RET_QK_WIDTH + 2 * RET_V_WIDTH + 2 * D_MODEL
FFN_HIDDEN = -(-8 * D_MODEL // (3 * 256)) * 256
LN_EPS = 1e-5
GN_EPS = 1e-6
DEEPNORM_ALPHA = (2 * DEPTH) ** 0.25
DEEPNORM_BETA = (8 * DEPTH) ** -0.25
NEG_INF = -1e30

kernel_name = 'hybrid_dilated_attn_retention_deepnorm_block'


def _layer_norm(t, g, b):
    tf = t.astype(jnp.float32)
    mu = jnp.mean(tf, axis=-1, keepdims=True)
    var = jnp.mean(jnp.square(tf - mu), axis=-1, keepdims=True)
    return ((tf - mu) * lax.rsqrt(var + LN_EPS) * g + b).astype(t.dtype)


def _dilated_window_attention(q, k, v, window, dilation):
    b, s, h, e = q.shape
    w_sub = window // dilation
    span = dilation * ATT_BLOCK
    s_pad = -(-s // span) * span
    n_blk = s_pad // span
    pad = ((0, 0), (0, s_pad - s), (0, 0), (0, 0))

    def blocks(t):
        return jnp.pad(t, pad).reshape(b, n_blk, ATT_BLOCK, dilation, h, e)

    def with_prev(t):
        prev = jnp.pad(t[:, :-1], ((0, 0), (1, 0), (0, 0), (0, 0), (0, 0), (0, 0)))
        return jnp.concatenate([prev, t], axis=2)

    qb = blocks(q)
    kk = with_prev(blocks(k))
    vv = with_prev(blocks(v))
    qi = jnp.arange(ATT_BLOCK)[:, None]
    ki = jnp.arange(2 * ATT_BLOCK)[None, :]
    dist = qi + ATT_BLOCK - ki
    nb = jnp.arange(n_blk)[:, None, None]
    valid = (dist >= 0) & (dist <= w_sub) & (nb * ATT_BLOCK + ki - ATT_BLOCK >= 0)
    scores = jnp.einsum('bnqrhe,bnkrhe->bnrhqk', qb, kk).astype(jnp.float32) * (e ** -0.5)
    scores = jnp.where(valid[None, :, None, None], scores, NEG_INF)
    m = jnp.max(scores, axis=-1, keepdims=True)
    pr = jnp.exp(scores - m)
    denom = jnp.sum(pr, axis=-1)
    o = jnp.einsum('bnrhqk,bnkrhe->bnqrhe', pr.astype(v.dtype), vv).astype(jnp.float32)
    o = o / jnp.transpose(denom, (0, 1, 4, 2, 3))[..., None]
    lse = jnp.transpose(m[..., 0] + jnp.log(denom), (0, 1, 4, 2, 3))
    o = o.reshape(b, s_pad, h, e)[:, :s]
    lse = lse.reshape(b, s_pad, h)[:, :s]
    return o, lse


def _rotary(t, pos):
    half = t.shape[-1] // 2
    inv_freq = RET_ROPE_BASE ** (-jnp.arange(half, dtype=jnp.float32) / half)
    ang = pos[:, None] * inv_freq[None, :]
    cos = jnp.cos(ang)[None, :, None, :]
    sin = jnp.sin(ang)[None, :, None, :]
    t1 = t[..., :half].astype(jnp.float32)
    t2 = t[..., half:].astype(jnp.float32)
    return jnp.concatenate([t1 * cos - t2 * sin, t1 * sin + t2 * cos], axis=-1).astype(t.dtype)


def _retention(q, k, v):
    b, s, h, dk = q.shape
    dv = v.shape[-1]
    c = RET_CHUNK
    nc = s // c
    log_gamma = jnp.log(1.0 - 2.0 ** (-5.0 - jnp.arange(h, dtype=jnp.float32)))
    idx = jnp.arange(c, dtype=jnp.float32)
    diff = idx[:, None] - idx[None, :]
    intra = jnp.where(diff >= 0, jnp.exp(jnp.maximum(diff, 0.0)[None] * log_gamma[:, None, None]), 0.0)
    cross_decay = jnp.exp((idx + 1.0)[None, :] * log_gamma[:, None])[..., None]
    state_decay = jnp.exp((c - 1.0 - idx)[None, :] * log_gamma[:, None])[..., None]
    chunk_decay = jnp.exp(c * log_gamma)[:, None, None]

    def chunks(t):
        return t.reshape(b, nc, c, h, t.shape[-1]).transpose(1, 0, 3, 2, 4)

    def step(state, qkv):
        qc, kc, vc = qkv
        att = jnp.einsum('bhnd,bhmd->bhnm', qc, kc) * intra
        inner = jnp.einsum('bhnm,bhme->bhne', att, vc)
        cross = jnp.einsum('bhnd,bhde->bhne', qc, state) * cross_decay
        new_state = chunk_decay * state + jnp.einsum('bhmd,bhme->bhde', kc * state_decay, vc)
        return new_state, inner + cross

    init = jnp.zeros((b, h, dk, dv), jnp.float32)
    _, out = lax.scan(step, init, (chunks(q), chunks(k), chunks(v)))
    return out.transpose(1, 0, 3, 2, 4).reshape(b, s, h, dv)


def setup_inputs(seed: int = 0) -> dict:
    key = jax.random.key(seed)
    ks = jax.random.split(key, 17)

    def nrm(k, shape, scale):
        return jax.random.normal(k, shape, jnp.float32) * scale

    return {
        'x': nrm(ks[0], (BATCH, SEQ, D_MODEL), 1.0),
        'p': nrm(ks[1], (DEPTH, BATCH, SEQ, PLE_DIM), 1.0),
        'w_in': nrm(ks[2], (DEPTH, D_MODEL, IN_WIDTH), D_MODEL ** -0.5),
        'w_attn_out': nrm(ks[3], (DEPTH, ATT_OUT_WIDTH, D_MODEL), ATT_OUT_WIDTH ** -0.5),
        'w_ret_out': nrm(ks[4], (DEPTH, RET_V_WIDTH, D_MODEL), RET_V_WIDTH ** -0.5),
        'ret_gn_g': 1.0 + nrm(ks[5], (DEPTH, RET_V_WIDTH), 0.02),
        'w_o': nrm(ks[6], (DEPTH, D_MODEL, D_MODEL), DEEPNORM_BETA * D_MODEL ** -0.5),
        'ln1_g': 1.0 + nrm(ks[7], (DEPTH, D_MODEL), 0.02),
        'ln1_b': nrm(ks[8], (DEPTH, D_MODEL), 0.02),
        'w_ffn_gate': nrm(ks[9], (DEPTH, D_MODEL, FFN_HIDDEN), D_MODEL ** -0.5),
        'w_ffn_up': nrm(ks[10], (DEPTH, D_MODEL, FFN_HIDDEN), D_MODEL ** -0.5),
        'w_ffn_down': nrm(ks[11], (DEPTH, FFN_HIDDEN, D_MODEL), DEEPNORM_BETA * FFN_HIDDEN ** -0.5),
        'w_ple_gate': nrm(ks[12], (DEPTH, D_MODEL, D_MODEL), D_MODEL ** -0.5),
        'w_ple_up': nrm(ks[13], (DEPTH, PLE_DIM, D_MODEL), DEEPNORM_BETA * PLE_DIM ** -0.5),
        'ln2_g': 1.0 + nrm(ks[14], (DEPTH, D_MODEL), 0.02),
        'ln2_b': nrm(ks[15], (DEPTH, D_MODEL), 0.02),
    }


def reference(x, p, w_in, w_attn_out, w_ret_out, ret_gn_g, w_o, ln1_g, ln1_b, w_ffn_gate, w_ffn_up,
              w_ffn_down, w_ple_gate, w_ple_up, ln2_g, ln2_b):
    b, s, _ = x.shape
    pos = jnp.arange(s, dtype=jnp.float32)
    widths = [ATT_QKV_WIDTH] * 3 + [RET_QK_WIDTH] * 2 + [RET_V_WIDTH] * 2 + [D_MODEL] * 2
    splits = np.cumsum(widths)[:-1].tolist()
    h = x
    for i in range(DEPTH):
        proj = h @ w_in[i]
        qa, ka, va, qr, kr, vr, gr, gate_att, gate_ret = jnp.split(proj, splits, axis=-1)

        qa = qa.reshape(b, s, N_ATT_GROUPS, ATT_HEADS_PER_GROUP, ATT_HEAD_DIM)
        ka = ka.reshape(b, s, N_ATT_GROUPS, ATT_HEADS_PER_GROUP, ATT_HEAD_DIM)
        va = va.reshape(b, s, N_ATT_GROUPS, ATT_HEADS_PER_GROUP, ATT_HEAD_DIM)
        outs, lses = [], []
        for g, (win, dil) in enumerate(ATT_GROUPS):
            o_g, lse_g = _dilated_window_attention(qa[:, :, g], ka[:, :, g], va[:, :, g], win, dil)
            outs.append(o_g)
            lses.append(lse_g)
        wts = jax.nn.softmax(jnp.stack(lses, axis=0), axis=0)
        o_att = jnp.sum(wts[..., None] * jnp.stack(outs, axis=0), axis=0)
        o_att = o_att.astype(x.dtype).reshape(b, s, ATT_OUT_WIDTH)
        y_att = o_att @ w_attn_out[i]

        qr = _rotary(qr.reshape(b, s, RET_HEADS, RET_QK_DIM), pos)
        kr = _rotary(kr.reshape(b, s, RET_HEADS, RET_QK_DIM), pos) * (RET_QK_DIM ** -0.5)
        vr = vr.reshape(b, s, RET_HEADS, RET_V_DIM)
        r = _retention(qr, kr, vr)
        mu = jnp.mean(r, axis=-1, keepdims=True)
        var = jnp.mean(jnp.square(r - mu), axis=-1, keepdims=True)
        r = ((r - mu) * lax.rsqrt(var + GN_EPS)).reshape(b, s, RET_V_WIDTH) * ret_gn_g[i]
        y_ret = (jax.nn.silu(gr) * r.astype(x.dtype)) @ w_ret_out[i]

        mixed = jax.nn.sigmoid(gate_att) * y_att + jax.nn.sigmoid(gate_ret) * y_ret
        h = _layer_norm(DEEPNORM_ALPHA * h + mixed @ w_o[i], ln1_g[i], ln1_b[i])

        ffn = (jax.nn.silu(h @ w_ffn_gate[i]) * (h @ w_ffn_up[i])) @ w_ffn_down[i]
        ple = jax.nn.sigmoid(h @ w_ple_gate[i]) * (p[i] @ w_ple_up[i])
        h = _layer_norm(DEEPNORM_ALPHA * h + ffn + ple, ln2_g[i], ln2_b[i])
    return h
```

```python
import numpy as np
import ml_dtypes
from contextlib import ExitStack
import concourse.bass as bass
import concourse.mybir as mybir
from concourse.alu_op_type import AluOpType as ALU
from concourse.bass_utils import run_bass_kernel_spmd

F32 = mybir.dt.float32
BF16 = mybir.dt.bfloat16
AF = mybir.ActivationFunctionType

D = 4096
S_OWN = 2048
TE = 4096
INW = 29696
FFN = 11008
PLE = 256
ALPHA = float(2.0 ** 0.25)
LN_EPS = 1e-5
GN_EPS = 1e-6
NEG = -30000.0
OFF_QA, OFF_KA, OFF_VA, OFF_QR, OFF_KR, OFF_VR, OFF_GR, OFF_GA, OFF_GT = (
    0, 3072, 6144, 9216, 11264, 13312, 17408, 21504, 25600)
ARENA_BYTES = 176 * 1024


class Buf:
    __slots__ = ("name", "ap", "writers", "readers", "slot")

    def __init__(self, name, ap):
        self.name = name
        self.ap = ap
        self.writers = {}
        self.readers = {}
        self.slot = None


class Op:
    __slots__ = ("eng", "fn", "deps", "signal", "ms", "dma", "slot", "semval")

    def __init__(self, eng, fn):
        self.eng = eng
        self.fn = fn
        self.deps = set()
        self.signal = False
        self.ms = 0
        self.dma = False
        self.slot = None
        self.semval = 0


ENGS = ("pe", "act", "dve", "pool", "sp")


class Prog:
    def __init__(self):
        self.q = {e: [] for e in ENGS}
        self.bufs = []
        self.slot_cnt = []
        self.slot_last = []
        self.next_slot = 0
        self.slot_base = 0
        self.sems = None

    def buf(self, name, ap):
        b = Buf(name, ap)
        self.bufs.append(b)
        return b

    def add(self, eng, fn, reads=(), writes=(), dma_buf=None, extra=()):
        op = Op(eng, fn)
        cand = []
        for b in reads:
            for w in b.writers.values():
                cand.append((w, True))
        for b in writes:
            for r in b.readers.values():
                cand.append((r, False))
        for d in extra:
            cand.append((d, True))
        for d, raw in cand:
            if d is op:
                continue
            if d.eng == eng and not d.dma:
                if eng in ("pe", "pool", "sp") or not raw:
                    continue
            op.deps.add(d)
            d.signal = True
        if dma_buf is not None:
            op.dma = True
            if dma_buf.slot is None:
                dma_buf.slot = self.next_slot
                self.next_slot += 1
                if dma_buf.slot >= len(self.slot_cnt):
                    self.slot_cnt.append(0)
                    self.slot_last.append(None)
            sl = dma_buf.slot
            op.slot = sl
            self.slot_cnt[sl] += 1
            op.semval = 16 * self.slot_cnt[sl]
            self.slot_last[sl] = op
        for b in reads:
            b.readers[eng] = op
        for b in writes:
            if b.readers:
                b.readers = {}
                b.writers = {}
            b.writers[eng] = op
        self.q[eng].append(op)
        return op

    def freeze_slots(self):
        self.slot_base = self.next_slot

    def barrier(self):
        lasts = {}
        for e in ("pe", "act", "dve"):
            lasts[e] = None
            for o in reversed(self.q[e]):
                if o.fn is not None:
                    lasts[e] = o
                    break
        dmas = [d for d in self.slot_last if d is not None]
        self.next_slot = self.slot_base
        for e in ENGS:
            op = Op(e, None)
            for e2, l in lasts.items():
                if l is not None and e2 != e:
                    op.deps.add(l)
                    l.signal = True
            for d in dmas:
                op.deps.add(d)
            self.q[e].append(op)
        for b in self.bufs:
            b.readers = {}
            b.writers = {}

    def matmul(self, bank, out, lbuf, lhsT, rbuf, rhs, start, stop):
        return self.add("pe", lambda e: e.matmul(out, lhsT, rhs, start=start, stop=stop),
                        reads=[lbuf, rbuf], writes=[bank])

    def transpose(self, bank, out, ibuf, in_, idbuf, ident):
        return self.add("pe", lambda e: e.transpose(out, in_, ident),
                        reads=[ibuf, idbuf], writes=[bank])

    def act(self, out, in_, func, reads, writes, bias=None, scale=None):
        kw = {}
        if bias is not None:
            kw["bias"] = bias
        if scale is not None:
            kw["scale"] = scale
        return self.add("act", lambda e: e.activation(out, in_, func, **kw), reads=reads, writes=writes)

    def tt(self, out, in0, in1, op, reads, writes):
        return self.add("dve", lambda e: e.tensor_tensor(out, in0, in1, op), reads=reads, writes=writes)

    def stt(self, out, in0, scalar, in1, op0, op1, reads, writes):
        return self.add("dve", lambda e: e.scalar_tensor_tensor(out, in0, scalar, in1, op0, op1),
                        reads=reads, writes=writes)

    def copy(self, eng, out, in_, reads, writes):
        if eng == "act":
            return self.add("act", lambda e: e.activation(out, in_, AF.Copy), reads=reads, writes=writes)
        return self.add("dve", lambda e: e.tensor_copy(out, in_), reads=reads, writes=writes)

    def load(self, q, dst_buf, dst_ap, src_ap):
        return self.add(q, lambda e: e.dma_start(out=dst_ap, in_=src_ap), writes=[dst_buf], dma_buf=dst_buf)

    def store(self, q, src_buf, src_ap, dst_ap):
        return self.add(q, lambda e: e.dma_start(out=dst_ap, in_=src_ap), reads=[src_buf], dma_buf=src_buf)

    def emit(self, nc, es):
        engsem = {}
        for e in ("pe", "act", "dve"):
            engsem[e] = es.enter_context(nc.semaphore("ms_" + e))
            n = 0
            for op in self.q[e]:
                if op.signal and op.fn is not None:
                    n += 1
                    op.ms = n
                elif op.signal:
                    op.ms = n
        sems = [es.enter_context(nc.semaphore("d%d" % i)) for i in range(len(self.slot_cnt))]
        block = es.enter_context(nc.Block())

        def run(ename, e):
            maxw = {}
            for op in self.q[ename]:
                for d in op.deps:
                    if d.dma:
                        sem, val, key = sems[d.slot], d.semval, ("d", d.slot)
                    else:
                        sem, val, key = engsem[d.eng], d.ms, ("e", d.eng)
                    if maxw.get(key, 0) >= val:
                        continue
                    maxw[key] = val
                    e.wait_ge(sem, val)
                if op.fn is None:
                    continue
                ins = op.fn(e)
                if op.dma:
                    ins.then_inc(sems[op.slot], 16)
                elif op.signal:
                    ins.then_inc(engsem[ename], 1)

        @block.tensor
        def _(e):
            run("pe", e)

        @block.scalar
        def _(e):
            run("act", e)

        @block.vector
        def _(e):
            run("dve", e)

        @block.gpsimd
        def _(e):
            run("pool", e)

        @block.sync
        def _(e):
            run("sp", e)


class Arena:
    def __init__(self, P, base_ap, nbytes):
        self.P = P
        self.base = base_ap
        self.nbytes = nbytes
        self.off = 0
        self.mark = 0
        self.uid = 0

    def alloc(self, name, shape, dtype):
        esz = 4 if dtype == F32 else 2
        n = 1
        for s in shape[1:]:
            n *= s
        nb = n * esz
        nb_al = (nb + 63) // 64 * 64
        assert self.off + nb_al <= self.nbytes, ("SBUF arena overflow", name, self.off, nb_al)
        ap = self.base[:, self.off // 2:(self.off + nb) // 2]
        if dtype == F32:
            ap = ap.bitcast(F32)
        if len(shape) == 3:
            ap = ap.rearrange("p (a b) -> p a b", b=shape[2])
        elif len(shape) == 4:
            ap = ap.rearrange("p (a b c) -> p a b c", b=shape[2], c=shape[3])
        self.off += nb_al
        self.uid += 1
        return self.P.buf("%s_%d" % (name, self.uid), ap)

    def set_mark(self):
        self.mark = self.off

    def reset(self):
        self.off = self.mark


def build_program(upto=99, debug_out=None):
    nc = bass.Bass("TRN2", target_bir_lowering=False)
    P = Prog()

    def din(name, shape, dt=F32):
        return nc.dram_tensor(name, list(shape), dt, kind="ExternalInput").ap()

    def dscr(name, shape, dt):
        kind = "ExternalOutput" if debug_out == name else "Internal"
        return nc.dram_tensor(name, list(shape), dt, kind=kind).ap()

    xT = din("xT", [D, TE])
    x_own = din("x_own", [S_OWN, D])
    pT_d = din("pT", [PLE, S_OWN])
    w_in = din("w_in", [D, INW])
    w_ao = din("w_attn_out", [1024, D])
    w_ro = din("w_ret_out", [D, D])
    w_o = din("w_o", [D, D])
    w_fg = din("w_ffn_gate", [D, FFN])
    w_fu = din("w_ffn_up", [D, FFN])
    w_fd = din("w_ffn_down", [FFN, D])
    w_pg = din("w_ple_gate", [D, D])
    w_pu = din("w_ple_up", [PLE, D])
    gng_d = din("gng", [128, D])
    ln1g_d = din("ln1g", [128, D])
    ln1b_d = din("ln1b", [128, D])
    ln2g_d = din("ln2g", [128, D])
    ln2b_d = din("ln2b", [128, D])
    cos_d = din("cosT", [128, TE])
    sin_d = din("sinT", [128, TE])
    ident_d = din("ident", [128, 128], BF16)
    ones_d = din("ones", [128, 128], BF16)
    maskcp_d = din("maskcp", [128, 256], BF16)
    maskfirst_d = din("maskfirst", [128, 128], BF16)
    intra_d = din("intraT", [128, 8 * 128])
    cdb_d = din("cdb", [128, 8 * 128])
    sdk_d = din("sdk", [128, 8])
    cdk_d = din("cdk", [128, 8])
    out_d = nc.dram_tensor("out", [S_OWN, D], F32, kind="ExternalOutput").ap()

    QaT = dscr("QaT", [3072, S_OWN], BF16)
    KaT = dscr("KaT", [3072, TE], BF16)
    Va = dscr("Va", [TE, 3072], BF16)
    QrT = dscr("QrT", [2048, S_OWN], BF16)
    KrT = dscr("KrT", [2048, TE], BF16)
    Vr = dscr("Vr", [TE, D], BF16)
    GG = dscr("GG", [S_OWN, D], F32)
    SigA = dscr("SigA", [D, S_OWN], F32)
    SigR = dscr("SigR", [D, S_OWN], F32)
    OaT = dscr("OaT", [1024, S_OWN], BF16)
    ZT = dscr("ZT", [D, S_OWN], BF16)
    MIXT = dscr("MIXT", [D, S_OWN], BF16)
    PRE1 = dscr("PRE1", [S_OWN, D], F32)
    H1 = dscr("H1", [S_OWN, D], F32)
    H1T = dscr("H1T", [D, S_OWN], BF16)
    HIDT = dscr("HIDT", [FFN, S_OWN], BF16)
    PRE2 = dscr("PRE2", [S_OWN, D], F32)
    PRE3 = dscr("PRE3", [S_OWN, D], F32)

    es = ExitStack()
    arena_t = es.enter_context(nc.sbuf_tensor("arena", [128, ARENA_BYTES // 2], BF16))
    banks = []
    for i in range(8):
        pt = es.enter_context(nc.psum_tensor("ps%d" % i, [128, 512], F32))
        banks.append(P.buf("bank%d" % i, pt[:]))
    A = Arena(P, arena_t[:], ARENA_BYTES)

    ident = A.alloc("ident", [128, 128], BF16)
    ones = A.alloc("ones", [128, 128], BF16)
    maskcp = A.alloc("maskcp", [128, 256], BF16)
    maskfirst = A.alloc("maskfirst", [128, 128], BF16)
    sdk = A.alloc("sdk", [128, 8], F32)
    cdk = A.alloc("cdk", [128, 8], F32)
    epsg = A.alloc("epsg", [128, 1], F32)
    epsl = A.alloc("epsl", [128, 1], F32)
    for b, d in ((ident, ident_d), (ones, ones_d), (maskcp, maskcp_d), (maskfirst, maskfirst_d),
                 (sdk, sdk_d), (cdk, cdk_d)):
        P.load("sp", b, b.ap, d)
    P.freeze_slots()
    P.add("dve", lambda e: e.memset(epsg.ap, GN_EPS), writes=[epsg])
    P.add("dve", lambda e: e.memset(epsl.ap, LN_EPS), writes=[epsl])
    A.set_mark()

    evac_rr = [0]

    def evac_eng():
        evac_rr[0] ^= 1
        return "act" if evac_rr[0] else "dve"

    bank_rr = [0]

    def gemm_job(ring, ring_pos, products):
        out_banks = []
        for pr in products:
            W, c0, ncols, KC, actb, orient = pr["W"], pr["c0"], pr["ncols"], pr["KC"], pr["act"], pr["orient"]
            nun = (ncols // 128) if orient == "F" else 4
            bl = []
            for _ in range(nun):
                bl.append(banks[bank_rr[0] % 8])
                bank_rr[0] += 1
            out_banks.append(bl)
            npieces = (KC + 7) // 8
            for pc in range(npieces):
                k0 = pc * 8
                nk = min(8, KC - k0)
                wb = ring[ring_pos[0] % len(ring)]
                ring_pos[0] += 1
                src = W[k0 * 128:(k0 + nk) * 128, c0:c0 + ncols].rearrange("(kc p) n -> p kc n", p=128)
                P.load("pool", wb, wb.ap[:, 0:nk, 0:ncols], src)
                for kk in range(nk):
                    kc = k0 + kk
                    st, sp_ = (kc == 0), (kc == KC - 1)
                    for u in range(nun):
                        if orient == "F":
                            P.matmul(bl[u], bl[u].ap[:, 0:512], wb, wb.ap[:, kk, u * 128:(u + 1) * 128],
                                     actb, actb.ap[:, kc, :], st, sp_)
                        else:
                            P.matmul(bl[u], bl[u].ap[:, 0:ncols], actb, actb.ap[:, kc, u * 128:(u + 1) * 128],
                                     wb, wb.ap[:, kk, 0:ncols], st, sp_)
        return out_banks

    def stage_inproj():
        ring = [A.alloc("w", [128, 8, 512], BF16) for _ in range(6)]
        rp = [0]
        actT = [A.alloc("xT", [128, 32, 512], BF16) for _ in range(2)]
        cosb = A.alloc("cos", [128, 512], F32)
        sinb = A.alloc("sin", [128, 512], F32)
        gng = A.alloc("gng", [128, D], F32)
        P.load("sp", gng, gng.ap, gng_d)
        stb = [A.alloc("stb", [128, 4, 512], BF16) for _ in range(2)]
        stf = [A.alloc("stf", [128, 4, 512], F32) for _ in range(2)]
        tmp = [A.alloc("tmp", [128, 512], F32) for _ in range(4)]
        srr = [0, 0]

        def jobs_for(te0):
            own = te0 >= 2048
            jl = []
            if own:
                for i in range(6):
                    jl.append(("QA", OFF_QA + i * 512, i * 512))
            for i in range(6):
                if own or i >= 4 or te0 >= 1536:
                    jl.append(("KA", OFF_KA + i * 512, i * 512))
            for i in range(6):
                if own or i >= 4 or te0 >= 1536:
                    jl.append(("VA", OFF_VA + i * 512, i * 512))
            if own:
                for i in range(4):
                    jl.append(("QR", OFF_QR + i * 512, i * 512))
            for i in range(4):
                jl.append(("KR", OFF_KR + i * 512, i * 512))
            for i in range(8):
                jl.append(("VR", OFF_VR + i * 512, i * 512))
            if own:
                for i in range(8):
                    jl.append(("GR", OFF_GR + i * 512, i * 512))
                for i in range(8):
                    jl.append(("GA", OFF_GA + i * 512, i * 512))
                for i in range(8):
                    jl.append(("GT", OFF_GT + i * 512, i * 512))
            return jl

        for ti, te0 in enumerate(range(0, TE, 512)):
            own = te0 >= 2048
            to0 = te0 - 2048
            ab = actT[ti % 2]
            P.load("pool", ab, ab.ap, xT[:, te0:te0 + 512].rearrange("(kc p) t -> p kc t", p=128))
            P.load("sp", cosb, cosb.ap, cos_d[:, te0:te0 + 512])
            P.load("sp", sinb, sinb.ap, sin_d[:, te0:te0 + 512])
            for (kind, wc0, lc0) in jobs_for(te0):
                orient = "T" if kind in ("VA", "VR", "GR") else "F"
                bl = gemm_job(ring, rp, [dict(W=w_in, c0=wc0, ncols=512, KC=32, act=ab, orient=orient)])[0]
                if kind in ("QA", "KA", "VA", "VR"):
                    st = stb[srr[0] % 2]
                    srr[0] += 1
                    for u in range(4):
                        P.copy(evac_eng(), st.ap[:, u, :], bl[u].ap, [bl[u]], [st])
                    if kind == "QA":
                        dst = QaT[lc0:lc0 + 512, to0:to0 + 512].rearrange("(s p) t -> p s t", p=128)
                    elif kind == "KA":
                        dst = KaT[lc0:lc0 + 512, te0:te0 + 512].rearrange("(s p) t -> p s t", p=128)
                    elif kind == "VA":
                        dst = Va[te0:te0 + 512, lc0:lc0 + 512].rearrange("(s p) c -> p s c", p=128)
                    else:
                        dst = Vr[te0:te0 + 512, lc0:lc0 + 512].rearrange("(s p) c -> p s c", p=128)
                    P.store("sp", st, st.ap, dst)
                elif kind in ("QR", "KR"):
                    st = stb[srr[0] % 2]
                    srr[0] += 1
                    for hp in range(2):
                        b1, b2 = bl[2 * hp], bl[2 * hp + 1]
                        t0, t1, t2, t3 = tmp
                        P.tt(t0.ap, b1.ap, cosb.ap, ALU.mult, [b1, cosb], [t0])
                        P.tt(t1.ap, b2.ap, sinb.ap, ALU.mult, [b2, sinb], [t1])
                        P.tt(st.ap[:, 2 * hp, :], t0.ap, t1.ap, ALU.subtract, [t0, t1], [st])
                        P.tt(t2.ap, b1.ap, sinb.ap, ALU.mult, [b1, sinb], [t2])
                        P.tt(t3.ap, b2.ap, cosb.ap, ALU.mult, [b2, cosb], [t3])
                        P.tt(st.ap[:, 2 * hp + 1, :], t2.ap, t3.ap, ALU.add, [t2, t3], [st])
                    if kind == "QR":
                        dst = QrT[lc0:lc0 + 512, to0:to0 + 512].rearrange("(s p) t -> p s t", p=128)
                    else:
                        dst = KrT[lc0:lc0 + 512, te0:te0 + 512].rearrange("(s p) t -> p s t", p=128)
                    P.store("sp", st, st.ap, dst)
                elif kind == "GR":
                    st = stf[srr[1] % 2]
                    srr[1] += 1
                    for u in range(4):
                        t = tmp[u % 2]
                        P.act(t.ap, bl[u].ap, AF.Silu, [bl[u]], [t])
                        P.tt(st.ap[:, u, :], t.ap, gng.ap[:, lc0:lc0 + 512], ALU.mult, [t, gng], [st])
                    dst = GG[to0:to0 + 512, lc0:lc0 + 512].rearrange("(s p) c -> p s c", p=128)
                    P.store("sp", st, st.ap, dst)
                else:
                    st = stf[srr[1] % 2]
                    srr[1] += 1
                    for u in range(4):
                        P.act(st.ap[:, u, :], bl[u].ap, AF.Sigmoid, [bl[u]], [st])
                    tgt = SigA if kind == "GA" else SigR
                    dst = tgt[lc0:lc0 + 512, to0:to0 + 512].rearrange("(s p) t -> p s t", p=128)
                    P.store("sp", st, st.ap, dst)

    def stage_attn():
        scale = float(128.0 ** -0.5)
        UD = [A.alloc("UD", [128, 2, S_OWN], F32) for _ in range(3)]
        KT = [A.alloc("KT", [128, TE], BF16) for _ in range(2)]
        QT = [A.alloc("QT", [128, S_OWN], BF16) for _ in range(2)]
        VV = [A.alloc("VV", [128, 32 * 128], BF16) for _ in range(2)]
        PT = [A.alloc("PT", [128, 256], BF16) for _ in range(4)]
        ob = [A.alloc("ob", [128, S_OWN], BF16) for _ in range(2)]
        rec = A.alloc("rec", [128, S_OWN], F32)
        li = 0
        pti = 0
        sbi = 0
        ubi = 0
        for h in range(8):
            for g, d in enumerate((1, 4, 16)):
                halo = 128 * d
                te_lo = 2048 - halo
                ntok = 2048 + halo
                nblk = 16 // d
                row0 = g * 1024 + h * 128
                kt, qt, vv = KT[li % 2], QT[li % 2], VV[li % 2]
                li += 1
                P.load("sp", kt, kt.ap[:, 0:ntok], KaT[row0:row0 + 128, te_lo:TE])
                P.load("sp", qt, qt.ap, QaT[row0:row0 + 128, :])
                vv4 = vv.ap[:, 0:(nblk + 1) * d * 128].rearrange("p (n r e) -> p n r e", r=d, e=128)
                for n in range(nblk + 1):
                    src = Va[te_lo + n * 128 * d: te_lo + (n + 1) * 128 * d, row0:row0 + 128].rearrange(
                        "(j r) e -> j r e", r=d)
                    P.load("sp", vv, vv4[:, n, :, :], src)
                udg = UD[g]
                ud4 = udg.ap
                for r in range(d):
                    prev_pt = None
                    for n in range(nblk + 1):
                        kcols = kt.ap[:, n * 128 * d + r:(n + 1) * 128 * d:d]
                        if n == 0:
                            q_lo, nq, mk, mkb = 0, 128, maskfirst.ap, maskfirst
                        elif n < nblk:
                            q_lo, nq, mk, mkb = (n - 1) * 128, 256, maskcp.ap, maskcp
                        else:
                            q_lo, nq, mk, mkb = (n - 1) * 128, 128, maskcp.ap[:, 0:128], maskcp
                        qcols = qt.ap[:, q_lo * d + r:(q_lo + nq) * d:d]
                        sb_ = banks[sbi % 4]
                        sbi += 1
                        P.matmul(sb_, sb_.ap[:, 0:nq], kt, kcols, qt, qcols, True, False)
                        P.matmul(sb_, sb_.ap[:, 0:nq], ident, ident.ap, mkb, mk, False, True)
                        pt = PT[pti % 4]
                        pti += 1
                        P.act(pt.ap[:, 0:nq], sb_.ap[:, 0:nq], AF.Exp, [sb_], [pt], scale=scale)
                        if n >= 1:
                            m = n
                            ppt, poff = prev_pt
                            ub = banks[4 + ubi % 4]
                            ubi += 1
                            P.matmul(ub, ub.ap[:, 0:128], vv, vv4[:, m - 1, r, :], ppt, ppt.ap[:, poff:poff + 128],
                                     True, False)
                            P.matmul(ub, ub.ap[:, 0:128], vv, vv4[:, m, r, :], pt, pt.ap[:, 0:128], False, True)
                            P.matmul(ub, ub.ap[:, 128:256], ones, ones.ap, ppt, ppt.ap[:, poff:poff + 128],
                                     True, False)
                            P.matmul(ub, ub.ap[:, 128:256], ones, ones.ap, pt, pt.ap[:, 0:128], False, True)
                            c_lo = (m - 1) * 128 * d + r
                            dst = ud4[:, :, c_lo:c_lo + 127 * d + 1:d]
                            src = ub.ap[:, 0:256].rearrange("p (a b) -> p a b", b=128)
                            P.copy(evac_eng(), dst, src, [ub], [udg])
                        prev_pt = (pt, 0 if n == 0 else 128)
            P.tt(UD[0].ap, UD[0].ap, UD[1].ap, ALU.add, [UD[0], UD[1]], [UD[0]])
            P.tt(UD[0].ap, UD[0].ap, UD[2].ap, ALU.add, [UD[0], UD[2]], [UD[0]])
            P.add("dve", lambda e: e.reciprocal(rec.ap, UD[0].ap[:, 1, :]), reads=[UD[0]], writes=[rec])
            o = ob[h % 2]
            P.tt(o.ap, UD[0].ap[:, 0, :], rec.ap, ALU.mult, [UD[0], rec], [o])
            P.store("sp", o, o.ap, OaT[h * 128:(h + 1) * 128, :])

    def stage_ret():
        intra = A.alloc("intra", [128, 8, 128], F32)
        cdb = A.alloc("cdb", [128, 8, 128], F32)
        P.load("sp", intra, intra.ap, intra_d.rearrange("p (h n) -> p h n", n=128))
        P.load("sp", cdb, cdb.ap, cdb_d.rearrange("p (h n) -> p h n", n=128))
        KT = A.alloc("rKT", [128, 2, TE], BF16)
        QT = A.alloc("rQT", [128, 2, S_OWN], BF16)
        V = A.alloc("rV", [128, 32, 512], BF16)
        G = A.alloc("rG", [128, 16, 512], F32)
        zT = [A.alloc("zT", [128, 4, S_OWN], BF16) for _ in range(1)]
        Sf = A.alloc("Sf", [128, 2, 512], F32)
        Sb = A.alloc("Sb", [128, 2, 512], BF16)
        Ksd = [A.alloc("Ksd", [128, 256], BF16) for _ in range(2)]
        attb = [A.alloc("att", [128, 128], BF16) for _ in range(2)]
        Qs = [A.alloc("Qs", [128, 2, 128], BF16) for _ in range(2)]
        rn = [A.alloc("rn", [128, 512], F32) for _ in range(2)]
        zb = [A.alloc("zb", [128, 512], BF16) for _ in range(2)]
        stats = [A.alloc("stats", [128, 6], F32) for _ in range(2)]
        mv = [A.alloc("mv", [128, 2], F32) for _ in range(2)]
        sd = [A.alloc("sd", [128, 1], F32) for _ in range(2)]
        rstd = [A.alloc("rstd", [128, 1], F32) for _ in range(2)]
        nmr = [A.alloc("nmr", [128, 1], F32) for _ in range(2)]
        bK, bS0, bS1 = banks[0], banks[1], banks[2]
        bA = [banks[3], banks[6]]
        bO = [banks[4], banks[7]]
        bZ = banks[5]
        for h in range(8):
            P.load("sp", KT, KT.ap, KrT[h * 256:(h + 1) * 256, :].rearrange("(dt p) t -> p dt t", p=128))
            P.load("sp", QT, QT.ap, QrT[h * 256:(h + 1) * 256, :].rearrange("(dt p) t -> p dt t", p=128))
            for c4 in range(4):
                P.load("sp", V, V.ap[:, c4 * 8:(c4 + 1) * 8, :],
                       Vr[c4 * 1024:(c4 + 1) * 1024, h * 512:(h + 1) * 512].rearrange("(c p) e -> p c e", p=128))
            for c4 in range(2):
                P.load("sp", G, G.ap[:, c4 * 8:(c4 + 1) * 8, :],
                       GG[c4 * 1024:(c4 + 1) * 1024, h * 512:(h + 1) * 512].rearrange("(c p) e -> p c e", p=128))
            P.add("dve", lambda e: e.memset(Sf.ap, 0.0), writes=[Sf])
            P.add("dve", lambda e: e.memset(Sb.ap, 0.0), writes=[Sb])
            zt = zT[0]
            pending = None

            def flush(pend):
                zb_, co_ = pend
                bzv = bZ.ap.bitcast(BF16)
                for et in range(4):
                    P.transpose(bZ, bzv[:, et * 128:(et + 1) * 128], zb_, zb_.ap[:, et * 128:(et + 1) * 128],
                                ident, ident.ap)
                P.copy(evac_eng(), zt.ap[:, :, co_ * 128:(co_ + 1) * 128],
                       bzv[:, 0:512].rearrange("p (a b) -> p a b", b=128), [bZ], [zt])

            for c in range(32):
                own = c >= 16
                co = c - 16
                i2 = c % 2
                bkv = bK.ap.bitcast(BF16)
                for dt in range(2):
                    P.transpose(bK, bkv[:, dt * 128:(dt + 1) * 128], KT, KT.ap[:, dt, c * 128:(c + 1) * 128],
                                ident, ident.ap)
                ks = Ksd[i2]
                P.act(ks.ap, bkv[:, 0:256], AF.Identity, [bK, sdk], [ks], scale=sdk.ap[:, h:h + 1])
                if own:
                    ba, bo = bA[i2], bO[i2]
                    for dt in range(2):
                        P.matmul(ba, ba.ap[:, 0:128], KT, KT.ap[:, dt, c * 128:(c + 1) * 128],
                                 QT, QT.ap[:, dt, co * 128:(co + 1) * 128], dt == 0, dt == 1)
                    at = attb[i2]
                    P.tt(at.ap, ba.ap[:, 0:128], intra.ap[:, h, :], ALU.mult, [ba, intra], [at])
                    qs = Qs[i2]
                    for dt in range(2):
                        P.tt(qs.ap[:, dt, :], QT.ap[:, dt, co * 128:(co + 1) * 128], cdb.ap[:, h, :], ALU.mult,
                             [QT, cdb], [qs])
                    P.matmul(bo, bo.ap, at, at.ap, V, V.ap[:, c, :], True, False)
                    for dt in range(2):
                        P.matmul(bo, bo.ap, qs, qs.ap[:, dt, :], Sb, Sb.ap[:, dt, :], False, dt == 1)
                for dt, bs in enumerate((bS0, bS1)):
                    P.matmul(bs, bs.ap, ks, ks.ap[:, dt * 128:(dt + 1) * 128], V, V.ap[:, c, :], True, True)
                if pending is not None:
                    flush(pending)
                    pending = None
                if c < 31:
                    for dt, bs in enumerate((bS0, bS1)):
                        P.stt(Sf.ap[:, dt, :], Sf.ap[:, dt, :], cdk.ap[:, h:h + 1], bs.ap, ALU.mult, ALU.add,
                              [Sf, cdk, bs], [Sf])
                    P.copy("act", Sb.ap, Sf.ap, [Sf], [Sb])
                if own:
                    st_, mv_, sd_, rs_, nm_ = stats[i2], mv[i2], sd[i2], rstd[i2], nmr[i2]
                    P.add("dve", lambda e, a=st_.ap, b=bo.ap: e.bn_stats(a, b), reads=[bo], writes=[st_])
                    P.add("dve", lambda e, a=mv_.ap, b=st_.ap: e.bn_aggr(a, b), reads=[st_], writes=[mv_])
                    P.act(sd_.ap, mv_.ap[:, 1:2], AF.Sqrt, [mv_, epsg], [sd_], bias=epsg.ap, scale=1.0)
                    P.add("dve", lambda e, a=rs_.ap, b=sd_.ap: e.reciprocal(a, b), reads=[sd_], writes=[rs_])
                    P.stt(nm_.ap, mv_.ap[:, 0:1], -1.0, rs_.ap, ALU.mult, ALU.mult, [mv_, rs_], [nm_])
                    r_ = rn[i2]
                    P.act(r_.ap, bo.ap, AF.Identity, [bo, rs_, nm_], [r_], bias=nm_.ap, scale=rs_.ap)
                    z_ = zb[i2]
                    P.tt(z_.ap, r_.ap, G.ap[:, co, :], ALU.mult, [r_, G], [z_])
                    pending = (z_, co)
            if pending is not None:
                flush(pending)
            P.store("sp", zt, zt.ap, ZT[h * 512:(h + 1) * 512, :].rearrange("(et p) t -> p et t", p=128))

    def stage_mix():
        ring = [A.alloc("w", [128, 8, 512], BF16) for _ in range(5)]
        rp = [0]
        oa = [A.alloc("oaT", [128, 8, 512], BF16) for _ in range(2)]
        za = [A.alloc("zaT", [128, 32, 512], BF16) for _ in range(2)]
        sga = [A.alloc("sga", [128, 4, 512], F32) for _ in range(2)]
        sgr = [A.alloc("sgr", [128, 4, 512], F32) for _ in range(2)]
        stb = [A.alloc("stb", [128, 4, 512], BF16) for _ in range(2)]
        tmp = [A.alloc("tmp", [128, 512], F32) for _ in range(4)]
        j = 0
        for ti in range(4):
            t0 = ti * 512
            ob_, zb_ = oa[ti % 2], za[ti % 2]
            P.load("sp", ob_, ob_.ap, OaT[:, t0:t0 + 512].rearrange("(kc p) t -> p kc t", p=128))
            P.load("sp", zb_, zb_.ap, ZT[:, t0:t0 + 512].rearrange("(kc p) t -> p kc t", p=128))
            for cb in range(8):
                c0 = cb * 512
                sa, sr, st = sga[j % 2], sgr[j % 2], stb[j % 2]
                j += 1
                P.load("sp", sa, sa.ap, SigA[c0:c0 + 512, t0:t0 + 512].rearrange("(s p) t -> p s t", p=128))
                P.load("sp", sr, sr.ap, SigR[c0:c0 + 512, t0:t0 + 512].rearrange("(s p) t -> p s t", p=128))
                ba, br = gemm_job(ring, rp, [
                    dict(W=w_ao, c0=c0, ncols=512, KC=8, act=ob_, orient="F"),
                    dict(W=w_ro, c0=c0, ncols=512, KC=32, act=zb_, orient="F")])
                for u in range(4):
                    ta, tr = tmp[2 * (u % 2)], tmp[2 * (u % 2) + 1]
                    P.tt(ta.ap, ba[u].ap, sa.ap[:, u, :], ALU.mult, [ba[u], sa], [ta])
                    P.tt(tr.ap, br[u].ap, sr.ap[:, u, :], ALU.mult, [br[u], sr], [tr])
                    P.tt(st.ap[:, u, :], ta.ap, tr.ap, ALU.add, [ta, tr], [st])
                P.store("sp", st, st.ap, MIXT[c0:c0 + 512, t0:t0 + 512].rearrange("(s p) t -> p s t", p=128))

    def stage_wo():
        ring = [A.alloc("w", [128, 8, 512], BF16) for _ in range(6)]
        rp = [0]
        ma = [A.alloc("mT", [128, 32, 512], BF16) for _ in range(2)]
        xt = [A.alloc("xt", [128, 4, 512], F32) for _ in range(2)]
        stf = [A.alloc("stf", [128, 4, 512], F32) for _ in range(2)]
        j = 0
        for ti in range(4):
            t0 = ti * 512
            mb = ma[ti % 2]
            P.load("sp", mb, mb.ap, MIXT[:, t0:t0 + 512].rearrange("(kc p) t -> p kc t", p=128))
            for cb in range(8):
                c0 = cb * 512
                xb, st = xt[j % 2], stf[j % 2]
                j += 1
                P.load("sp", xb, xb.ap, x_own[t0:t0 + 512, c0:c0 + 512].rearrange("(s p) c -> p s c", p=128))
                bl = gemm_job(ring, rp, [dict(W=w_o, c0=c0, ncols=512, KC=32, act=mb, orient="T")])[0]
                for u in range(4):
                    P.stt(st.ap[:, u, :], xb.ap[:, u, :], ALPHA, bl[u].ap, ALU.mult, ALU.add, [xb, bl[u]], [st])
                P.store("sp", st, st.ap, PRE1[t0:t0 + 512, c0:c0 + 512].rearrange("(s p) c -> p s c", p=128))

    def stage_ln(src, g_d, b_d, dst_f32, dst_T):
        gt = A.alloc("lng", [128, D], F32)
        bt = A.alloc("lnb", [128, D], F32)
        P.load("sp", gt, gt.ap, g_d)
        P.load("sp", bt, bt.ap, b_d)
        xr = [A.alloc("xr", [128, D], F32) for _ in range(2)]
        yr = [A.alloc("yr", [128, D], F32) for _ in range(2)]
        stats = [A.alloc("stats", [128, 8, 6], F32) for _ in range(2)]
        mv = [A.alloc("mv", [128, 2], F32) for _ in range(2)]
        sd = [A.alloc("sd", [128, 1], F32) for _ in range(2)]
        rstd = [A.alloc("rstd", [128, 1], F32) for _ in range(2)]
        nmr = [A.alloc("nmr", [128, 1], F32) for _ in range(2)]
        if dst_T is not None:
            hb = [A.alloc("hb", [128, D], BF16) for _ in range(2)]
            hT = [A.alloc("hT", [128, 32, 512], BF16) for _ in range(1)]
        for rt in range(16):
            i2 = rt % 2
            x_, y_ = xr[i2], yr[i2]
            P.load("sp", x_, x_.ap, src[rt * 128:(rt + 1) * 128, :])
            st_, mv_, sd_, rs_, nm_ = stats[i2], mv[i2], sd[i2], rstd[i2], nmr[i2]
            for k in range(8):
                P.add("dve", lambda e, a=st_.ap[:, k, :], b=x_.ap[:, k * 512:(k + 1) * 512]: e.bn_stats(a, b),
                      reads=[x_], writes=[st_])
            P.add("dve", lambda e, a=mv_.ap, b=st_.ap.rearrange("p a b -> p (a b)"): e.bn_aggr(a, b),
                  reads=[st_], writes=[mv_])
            P.act(sd_.ap, mv_.ap[:, 1:2], AF.Sqrt, [mv_, epsl], [sd_], bias=epsl.ap, scale=1.0)
            P.add("dve", lambda e, a=rs_.ap, b=sd_.ap: e.reciprocal(a, b), reads=[sd_], writes=[rs_])
            P.stt(nm_.ap, mv_.ap[:, 0:1], -1.0, rs_.ap, ALU.mult, ALU.mult, [mv_, rs_], [nm_])
            P.act(y_.ap, x_.ap, AF.Identity, [x_, rs_, nm_], [y_], bias=nm_.ap, scale=rs_.ap)
            P.tt(y_.ap, y_.ap, gt.ap, ALU.mult, [y_, gt], [y_])
            P.tt(y_.ap, y_.ap, bt.ap, ALU.add, [y_, bt], [y_])
            P.store("sp", y_, y_.ap, dst_f32[rt * 128:(rt + 1) * 128, :])
            if dst_T is not None:
                hb_ = hb[i2]
                P.copy("act", hb_.ap, y_.ap, [y_], [hb_])
                ht = hT[0]
                sub = rt % 4
                for k4 in range(4):
                    bk = banks[(rt * 4 + k4) % 8]
                    bkv = bk.ap.bitcast(BF16)
                    for kk in range(8):
                        kc = k4 * 8 + kk
                        P.transpose(bk, bkv[:, kk * 128:(kk + 1) * 128], hb_, hb_.ap[:, kc * 128:(kc + 1) * 128],
                                    ident, ident.ap)
                    P.copy(evac_eng(), ht.ap[:, k4 * 8:(k4 + 1) * 8, sub * 128:(sub + 1) * 128],
                           bkv.rearrange("p (a b) -> p a b", b=128), [bk], [ht])
                if sub == 3:
                    t0 = (rt // 4) * 512
                    P.store("sp", ht, ht.ap, dst_T[:, t0:t0 + 512].rearrange("(kc p) t -> p kc t", p=128))

    def stage_ffn_up():
        ring = [A.alloc("w", [128, 8, 512], BF16) for _ in range(8)]
        rp = [0]
        ha = [A.alloc("hT", [128, 32, 512], BF16) for _ in range(2)]
        stb = [A.alloc("stb", [128, 4, 512], BF16) for _ in range(2)]
        tmp = [A.alloc("tmp", [128, 512], F32) for _ in range(2)]
        j = 0
        for ti in range(4):
            t0 = ti * 512
            hb_ = ha[ti % 2]
            P.load("sp", hb_, hb_.ap, H1T[:, t0:t0 + 512].rearrange("(kc p) t -> p kc t", p=128))
            for c0 in range(0, FFN, 512):
                ncols = min(512, FFN - c0)
                nsub = ncols // 128
                st = stb[j % 2]
                j += 1
                bg, bu = gemm_job(ring, rp, [
                    dict(W=w_fg, c0=c0, ncols=ncols, KC=32, act=hb_, orient="F"),
                    dict(W=w_fu, c0=c0, ncols=ncols, KC=32, act=hb_, orient="F")])
                for u in range(nsub):
                    t = tmp[u % 2]
                    P.act(t.ap, bg[u].ap, AF.Silu, [bg[u]], [t])
                    P.tt(st.ap[:, u, :], t.ap, bu[u].ap, ALU.mult, [t, bu[u]], [st])
                P.store("sp", st, st.ap[:, 0:nsub, :],
                        HIDT[c0:c0 + ncols, t0:t0 + 512].rearrange("(s p) t -> p s t", p=128))

    def stage_ple():
        ring = [A.alloc("w", [128, 8, 512], BF16) for _ in range(6)]
        rp = [0]
        ha = [A.alloc("hT", [128, 32, 512], BF16) for _ in range(2)]
        pa = [A.alloc("pT", [128, 2, 512], BF16) for _ in range(2)]
        h1t = [A.alloc("h1t", [128, 4, 512], F32) for _ in range(2)]
        stf = [A.alloc("stf", [128, 4, 512], F32) for _ in range(2)]
        tmp = [A.alloc("tmp", [128, 512], F32) for _ in range(4)]
        j = 0
        for ti in range(4):
            t0 = ti * 512
            hb_, pb_ = ha[ti % 2], pa[ti % 2]
            P.load("sp", hb_, hb_.ap, H1T[:, t0:t0 + 512].rearrange("(kc p) t -> p kc t", p=128))
            P.load("pool", pb_, pb_.ap, pT_d[:, t0:t0 + 512].rearrange("(kc p) t -> p kc t", p=128))
            for cb in range(8):
                c0 = cb * 512
                hx, st = h1t[j % 2], stf[j % 2]
                j += 1
                P.load("sp", hx, hx.ap, H1[t0:t0 + 512, c0:c0 + 512].rearrange("(s p) c -> p s c", p=128))
                bg, bu = gemm_job(ring, rp, [
                    dict(W=w_pg, c0=c0, ncols=512, KC=32, act=hb_, orient="T"),
                    dict(W=w_pu, c0=c0, ncols=512, KC=2, act=pb_, orient="T")])
                for u in range(4):
                    ta, tb = tmp[2 * (u % 2)], tmp[2 * (u % 2) + 1]
                    P.act(ta.ap, bg[u].ap, AF.Sigmoid, [bg[u]], [ta])
                    P.tt(tb.ap, ta.ap, bu[u].ap, ALU.mult, [ta, bu[u]], [tb])
                    P.stt(st.ap[:, u, :], hx.ap[:, u, :], ALPHA, tb.ap, ALU.mult, ALU.add, [hx, tb], [st])
                P.store("sp", st, st.ap, PRE2[t0:t0 + 512, c0:c0 + 512].rearrange("(s p) c -> p s c", p=128))

    def stage_ffn_down():
        KC = FFN // 128
        ring = [A.alloc("w", [128, 8, 512], BF16) for _ in range(6)]
        rp = [0]
        ha = [A.alloc("hidT", [128, KC, 512], BF16) for _ in range(1)]
        prt = [A.alloc("prt", [128, 4, 512], F32) for _ in range(2)]
        stf = [A.alloc("stf", [128, 4, 512], F32) for _ in range(2)]
        j = 0
        for ti in range(4):
            t0 = ti * 512
            hb_ = ha[0]
            for k0 in range(0, KC, 16):
                k1 = min(KC, k0 + 16)
                P.load("sp", hb_, hb_.ap[:, k0:k1, :],
                       HIDT[k0 * 128:k1 * 128, t0:t0 + 512].rearrange("(kc p) t -> p kc t", p=128))
            for cb in range(8):
                c0 = cb * 512
                px, st = prt[j % 2], stf[j % 2]
                j += 1
                P.load("sp", px, px.ap, PRE2[t0:t0 + 512, c0:c0 + 512].rearrange("(s p) c -> p s c", p=128))
                bl = gemm_job(ring, rp, [dict(W=w_fd, c0=c0, ncols=512, KC=KC, act=hb_, orient="T")])[0]
                for u in range(4):
                    P.tt(st.ap[:, u, :], px.ap[:, u, :], bl[u].ap, ALU.add, [px, bl[u]], [st])
                P.store("sp", st, st.ap, PRE3[t0:t0 + 512, c0:c0 + 512].rearrange("(s p) c -> p s c", p=128))

    stages = [
        ("inproj", stage_inproj),
        ("attn", stage_attn),
        ("ret", stage_ret),
        ("mix", stage_mix),
        ("wo", stage_wo),
        ("ln1", lambda: stage_ln(PRE1, ln1g_d, ln1b_d, H1, H1T)),
        ("ffn_up", stage_ffn_up),
        ("ple", stage_ple),
        ("ffn_down", stage_ffn_down),
        ("ln2", lambda: stage_ln(PRE3, ln2g_d, ln2b_d, out_d, None)),
    ]
    for i, (name, fn) in enumerate(stages):
        if i >= upto:
            break
        P.barrier()
        A.reset()
        fn()
    P.barrier()
    P.emit(nc, es)
    es.close()
    return nc


def _const_tables(hf):
    bf = ml_dtypes.bfloat16
    t = {}
    t["ident"] = np.eye(128, dtype=np.float32).astype(bf)
    t["ones"] = np.ones((128, 128), np.float32).astype(bf)
    k = np.arange(128)[:, None]
    q = np.arange(128)[None, :]
    cur = np.where(q >= k, 0.0, NEG).astype(np.float32)
    prev = np.where(k >= q, 0.0, NEG).astype(np.float32)
    t["maskcp"] = np.concatenate([cur, prev], axis=1).astype(bf)
    t["maskfirst"] = (prev if hf == 1 else np.full((128, 128), NEG, np.float32)).astype(bf)
    hh = np.arange(8, dtype=np.float64)
    lg = np.log(1.0 - 2.0 ** (-5.0 - hh))
    m = np.arange(128)[:, None].astype(np.float64)
    n = np.arange(128)[None, :].astype(np.float64)
    intra = np.zeros((128, 8, 128), np.float64)
    cdbt = np.zeros((128, 8, 128), np.float64)
    for h in range(8):
        intra[:, h, :] = np.where(n >= m, np.exp(np.maximum(n - m, 0.0) * lg[h]), 0.0) / 16.0
        cdbt[:, h, :] = np.exp((n + 1.0) * lg[h])
    t["intraT"] = intra.reshape(128, 1024).astype(np.float32)
    t["cdb"] = cdbt.reshape(128, 1024).astype(np.float32)
    t["sdk"] = (np.exp((127.0 - m) * lg[None, :]) / 16.0).astype(np.float32)
    t["cdk"] = np.broadcast_to(np.exp(128.0 * lg)[None, :], (128, 8)).astype(np.float32).copy()
    half = 128
    inv_freq = (np.float32(10000.0) ** (-(np.arange(half, dtype=np.float32)) / np.float32(half))).astype(np.float32)
    pos = (hf * 2048 - 2048 + np.arange(TE)).astype(np.float32)
    ang = (inv_freq[:, None] * pos[None, :]).astype(np.float32)
    t["cosT"] = np.cos(ang.astype(np.float64)).astype(np.float32)
    t["sinT"] = np.sin(ang.astype(np.float64)).astype(np.float32)
    return t


def make_in_maps(x, p, w_in, w_attn_out, w_ret_out, ret_gn_g, w_o, ln1_g, ln1_b, w_ffn_gate, w_ffn_up,
                 w_ffn_down, w_ple_gate, w_ple_up, ln2_g, ln2_b):
    f = lambda a: np.ascontiguousarray(np.asarray(a, dtype=np.float32))
    x = f(x)
    p = f(p)
    shared = {
        "w_in": f(w_in)[0], "w_attn_out": f(w_attn_out)[0], "w_ret_out": f(w_ret_out)[0], "w_o": f(w_o)[0],
        "w_ffn_gate": f(w_ffn_gate)[0], "w_ffn_up": f(w_ffn_up)[0], "w_ffn_down": f(w_ffn_down)[0],
        "w_ple_gate": f(w_ple_gate)[0], "w_ple_up": f(w_ple_up)[0],
        "gng": np.ascontiguousarray(np.broadcast_to(f(ret_gn_g)[0][None, :], (128, D))),
        "ln1g": np.ascontiguousarray(np.broadcast_to(f(ln1_g)[0][None, :], (128, D))),
        "ln1b": np.ascontiguousarray(np.broadcast_to(f(ln1_b)[0][None, :], (128, D))),
        "ln2g": np.ascontiguousarray(np.broadcast_to(f(ln2_g)[0][None, :], (128, D))),
        "ln2b": np.ascontiguousarray(np.broadcast_to(f(ln2_b)[0][None, :], (128, D))),
    }
    tabs = [_const_tables(0), _const_tables(1)]
    in_maps = []
    for c in range(8):
        b, hf = c // 2, c % 2
        own = x[b, hf * 2048:(hf + 1) * 2048]
        if hf == 1:
            ext = x[b]
        else:
            ext = np.concatenate([np.zeros_like(own), own], axis=0)
        m = dict(shared)
        m["xT"] = np.ascontiguousarray(ext.T)
        m["x_own"] = np.ascontiguousarray(own)
        m["pT"] = np.ascontiguousarray(p[0, b, hf * 2048:(hf + 1) * 2048].T)
        m.update(tabs[hf])
        in_maps.append(m)
    return in_maps


_NC_CACHE = {}


def kernel(**inputs):
    in_maps = make_in_maps(**inputs)
    if "nc" not in _NC_CACHE:
        _NC_CACHE["nc"] = build_program()
    nc = _NC_CACHE["nc"]
    res = run_bass_kernel_spmd(nc, in_maps, core_ids=list(range(8)))
    out = np.empty((4, 4096, D), np.float32)
    for c in range(8):
        b, hf = c // 2, c % 2
        out[b, hf * 2048:(hf + 1) * 2048] = res.results[c]["out"]
    return out
```

```python
import numpy as np
import ml_dtypes
from contextlib import ExitStack
import concourse.bass as bass
import concourse.mybir as mybir
from concourse.alu_op_type import AluOpType as ALU
from concourse.bass_utils import run_bass_kernel_spmd

F32 = mybir.dt.float32
BF16 = mybir.dt.bfloat16
AF = mybir.ActivationFunctionType

D = 4096
S_OWN = 2048
TE = 4096
INW = 29696
FFN = 11008
PLE = 256
ALPHA = float(2.0 ** 0.25)
LN_EPS = 1e-5
GN_EPS = 1e-6
NEG = -30000.0
OFF_QA, OFF_KA, OFF_VA, OFF_QR, OFF_KR, OFF_VR, OFF_GR, OFF_GA, OFF_GT = (
    0, 3072, 6144, 9216, 11264, 13312, 17408, 21504, 25600)
ARENA_BYTES = 176 * 1024


class Buf:
    __slots__ = ("name", "ap", "writers", "readers", "slot")

    def __init__(self, name, ap):
        self.name = name
        self.ap = ap
        self.writers = {}
        self.readers = {}
        self.slot = None


class Op:
    __slots__ = ("eng", "fn", "deps", "signal", "ms", "dma", "slot", "semval")

    def __init__(self, eng, fn):
        self.eng = eng
        self.fn = fn
        self.deps = set()
        self.signal = False
        self.ms = 0
        self.dma = False
        self.slot = None
        self.semval = 0


ENGS = ("pe", "act", "dve", "pool", "sp")


class Prog:
    def __init__(self):
        self.q = {e: [] for e in ENGS}
        self.bufs = []
        self.slot_cnt = []
        self.slot_last = []
        self.next_slot = 0
        self.slot_base = 0
        self.sems = None

    def buf(self, name, ap):
        b = Buf(name, ap)
        self.bufs.append(b)
        return b

    def add(self, eng, fn, reads=(), writes=(), dma_buf=None, extra=()):
        op = Op(eng, fn)
        cand = []
        for b in reads:
            for w in b.writers.values():
                cand.append((w, True))
        for b in writes:
            for r in b.readers.values():
                cand.append((r, False))
        for d in extra:
            cand.append((d, True))
        for d, raw in cand:
            if d is op:
                continue
            if d.eng == eng and not d.dma:
                if eng in ("pe", "pool", "sp") or not raw:
                    continue
            op.deps.add(d)
            d.signal = True
        if dma_buf is not None:
            op.dma = True
            if dma_buf.slot is None:
                dma_buf.slot = self.next_slot
                self.next_slot += 1
                if dma_buf.slot >= len(self.slot_cnt):
                    self.slot_cnt.append(0)
                    self.slot_last.append(None)
            sl = dma_buf.slot
            op.slot = sl
            self.slot_cnt[sl] += 1
            op.semval = 16 * self.slot_cnt[sl]
            self.slot_last[sl] = op
        for b in reads:
            b.readers[eng] = op
        for b in writes:
            if b.readers:
                b.readers = {}
                b.writers = {}
            b.writers[eng] = op
        self.q[eng].append(op)
        return op

    def freeze_slots(self):
        self.slot_base = self.next_slot

    def barrier(self):
        lasts = {}
        for e in ("pe", "act", "dve"):
            lasts[e] = None
            for o in reversed(self.q[e]):
                if o.fn is not None:
                    lasts[e] = o
                    break
        dmas = [d for d in self.slot_last if d is not None]
        self.next_slot = self.slot_base
        for e in ENGS:
            op = Op(e, None)
            for e2, l in lasts.items():
                if l is not None and e2 != e:
                    op.deps.add(l)
                    l.signal = True
            for d in dmas:
                op.deps.add(d)
            self.q[e].append(op)
        for b in self.bufs:
            b.readers = {}
            b.writers = {}

    def matmul(self, bank, out, lbuf, lhsT, rbuf, rhs, start, stop):
        return self.add("pe", lambda e: e.matmul(out, lhsT, rhs, start=start, stop=stop),
                        reads=[lbuf, rbuf], writes=[bank])

    def transpose(self, bank, out, ibuf, in_, idbuf, ident):
        return self.add("pe", lambda e: e.transpose(out, in_, ident),
                        reads=[ibuf, idbuf], writes=[bank])

    def act(self, out, in_, func, reads, writes, bias=None, scale=None):
        kw = {}
        if bias is not None:
            kw["bias"] = bias
        if scale is not None:
            kw["scale"] = scale
        return self.add("act", lambda e: e.activation(out, in_, func, **kw), reads=reads, writes=writes)

    def tt(self, out, in0, in1, op, reads, writes):
        return self.add("dve", lambda e: e.tensor_tensor(out, in0, in1, op), reads=reads, writes=writes)

    def stt(self, out, in0, scalar, in1, op0, op1, reads, writes):
        return self.add("dve", lambda e: e.scalar_tensor_tensor(out, in0, scalar, in1, op0, op1),
                        reads=reads, writes=writes)

    def copy(self, eng, out, in_, reads, writes):
        if eng == "act":
            return self.add("act", lambda e: e.activation(out, in_, AF.Copy), reads=reads, writes=writes)
        return self.add("dve", lambda e: e.tensor_copy(out, in_), reads=reads, writes=writes)

    def load(self, q, dst_buf, dst_ap, src_ap):
        return self.add(q, lambda e: e.dma_start(out=dst_ap, in_=src_ap), writes=[dst_buf], dma_buf=dst_buf)

    def store(self, q, src_buf, src_ap, dst_ap):
        return self.add(q, lambda e: e.dma_start(out=dst_ap, in_=src_ap), reads=[src_buf], dma_buf=src_buf)

    def emit(self, nc, es):
        engsem = {}
        for e in ("pe", "act", "dve"):
            engsem[e] = es.enter_context(nc.semaphore("ms_" + e))
            n = 0
            for op in self.q[e]:
                if op.signal and op.fn is not None:
                    n += 1
                    op.ms = n
                elif op.signal:
                    op.ms = n
        sems = [es.enter_context(nc.semaphore("d%d" % i)) for i in range(len(self.slot_cnt))]
        block = es.enter_context(nc.Block())

        def run(ename, e):
            maxw = {}
            for op in self.q[ename]:
                for d in op.deps:
                    if d.dma:
                        sem, val, key = sems[d.slot], d.semval, ("d", d.slot)
                    else:
                        sem, val, key = engsem[d.eng], d.ms, ("e", d.eng)
                    if maxw.get(key, 0) >= val:
                        continue
                    maxw[key] = val
                    e.wait_ge(sem, val)
                if op.fn is None:
                    continue
                ins = op.fn(e)
                if op.dma:
                    ins.then_inc(sems[op.slot], 16)
                elif op.signal:
                    ins.then_inc(engsem[ename], 1)

        @block.tensor
        def _(e):
            run("pe", e)

        @block.scalar
        def _(e):
            run("act", e)

        @block.vector
        def _(e):
            run("dve", e)

        @block.gpsimd
        def _(e):
            run("pool", e)

        @block.sync
        def _(e):
            run("sp", e)


class Arena:
    def __init__(self, P, base_ap, nbytes):
        self.P = P
        self.base = base_ap
        self.nbytes = nbytes
        self.off = 0
        self.mark = 0
        self.uid = 0

    def alloc(self, name, shape, dtype):
        esz = 4 if dtype == F32 else 2
        n = 1
        for s in shape[1:]:
            n *= s
        nb = n * esz
        nb_al = (nb + 63) // 64 * 64
        assert self.off + nb_al <= self.nbytes, ("SBUF arena overflow", name, self.off, nb_al)
        ap = self.base[:, self.off // 2:(self.off + nb) // 2]
        if dtype == F32:
            ap = ap.bitcast(F32)
        if len(shape) == 3:
            ap = ap.rearrange("p (a b) -> p a b", b=shape[2])
        elif len(shape) == 4:
            ap = ap.rearrange("p (a b c) -> p a b c", b=shape[2], c=shape[3])
        self.off += nb_al
        self.uid += 1
        return self.P.buf("%s_%d" % (name, self.uid), ap)

    def set_mark(self):
        self.mark = self.off

    def reset(self):
        self.off = self.mark


def build_program(upto=99, debug_out=None):
    nc = bass.Bass("TRN2", target_bir_lowering=False)
    P = Prog()

    def din(name, shape, dt=F32):
        return nc.dram_tensor(name, list(shape), dt, kind="ExternalInput").ap()

    def dscr(name, shape, dt):
        kind = "ExternalOutput" if debug_out == name else "Internal"
        return nc.dram_tensor(name, list(shape), dt, kind=kind).ap()

    xT = din("xT", [D, TE])
    x_own = din("x_own", [S_OWN, D])
    pT_d = din("pT", [PLE, S_OWN])
    w_in = din("w_in", [D, INW])
    w_ao = din("w_attn_out", [1024, D])
    w_ro = din("w_ret_out", [D, D])
    w_o = din("w_o", [D, D])
    w_fg = din("w_ffn_gate", [D, FFN])
    w_fu = din("w_ffn_up", [D, FFN])
    w_fd = din("w_ffn_down", [FFN, D])
    w_pg = din("w_ple_gate", [D, D])
    w_pu = din("w_ple_up", [PLE, D])
    gng_d = din("gng", [128, D])
    ln1g_d = din("ln1g", [128, D])
    ln1b_d = din("ln1b", [128, D])
    ln2g_d = din("ln2g", [128, D])
    ln2b_d = din("ln2b", [128, D])
    cos_d = din("cosT", [128, TE])
    sin_d = din("sinT", [128, TE])
    ident_d = din("ident", [128, 128], BF16)
    ones_d = din("ones", [128, 128], BF16)
    maskcp_d = din("maskcp", [128, 256], BF16)
    maskfirst_d = din("maskfirst", [128, 128], BF16)
    intra_d = din("intraT", [128, 8 * 128])
    cdb_d = din("cdb", [128, 8 * 128])
    sdk_d = din("sdk", [128, 8])
    cdk_d = din("cdk", [128, 8])
    out_d = nc.dram_tensor("out", [S_OWN, D], F32, kind="ExternalOutput").ap()

    QaT = dscr("QaT", [3072, S_OWN], BF16)
    KaT = dscr("KaT", [3072, TE], BF16)
    Va = dscr("Va", [TE, 3072], BF16)
    QrT = dscr("QrT", [2048, S_OWN], BF16)
    KrT = dscr("KrT", [2048, TE], BF16)
    Vr = dscr("Vr", [TE, D], BF16)
    GG = dscr("GG", [S_OWN, D], F32)
    SigA = dscr("SigA", [D, S_OWN], F32)
    SigR = dscr("SigR", [D, S_OWN], F32)
    OaT = dscr("OaT", [1024, S_OWN], BF16)
    ZT = dscr("ZT", [D, S_OWN], BF16)
    MIXT = dscr("MIXT", [D, S_OWN], BF16)
    PRE1 = dscr("PRE1", [S_OWN, D], F32)
    H1 = dscr("H1", [S_OWN, D], F32)
    H1T = dscr("H1T", [D, S_OWN], BF16)
    HIDT = dscr("HIDT", [FFN, S_OWN], BF16)
    PRE2 = dscr("PRE2", [S_OWN, D], F32)
    PRE3 = dscr("PRE3", [S_OWN, D], F32)

    es = ExitStack()
    arena_t = es.enter_context(nc.sbuf_tensor("arena", [128, ARENA_BYTES // 2], BF16))
    banks = []
    for i in range(8):
        pt = es.enter_context(nc.psum_tensor("ps%d" % i, [128, 512], F32))
        banks.append(P.buf("bank%d" % i, pt[:]))
    A = Arena(P, arena_t[:], ARENA_BYTES)

    ident = A.alloc("ident", [128, 128], BF16)
    ones = A.alloc("ones", [128, 128], BF16)
    maskcp = A.alloc("maskcp", [128, 256], BF16)
    maskfirst = A.alloc("maskfirst", [128, 128], BF16)
    sdk = A.alloc("sdk", [128, 8], F32)
    cdk = A.alloc("cdk", [128, 8], F32)
    epsg = A.alloc("epsg", [128, 1], F32)
    epsl = A.alloc("epsl", [128, 1], F32)
    for b, d in ((ident, ident_d), (ones, ones_d), (maskcp, maskcp_d), (maskfirst, maskfirst_d),
                 (sdk, sdk_d), (cdk, cdk_d)):
        P.load("sp", b, b.ap, d)
    P.freeze_slots()
    P.add("dve", lambda e: e.memset(epsg.ap, GN_EPS), writes=[epsg])
    P.add("dve", lambda e: e.memset(epsl.ap, LN_EPS), writes=[epsl])
    A.set_mark()

    evac_rr = [0]

    def evac_eng():
        evac_rr[0] ^= 1
        return "act" if evac_rr[0] else "dve"

    bank_rr = [0]

    def gemm_job(ring, ring_pos, products):
        out_banks = []
        for pr in products:
            W, c0, ncols, KC, actb, orient = pr["W"], pr["c0"], pr["ncols"], pr["KC"], pr["act"], pr["orient"]
            nun = (ncols // 128) if orient == "F" else 4
            bl = []
            for _ in range(nun):
                bl.append(banks[bank_rr[0] % 8])
                bank_rr[0] += 1
            out_banks.append(bl)
            npieces = (KC + 7) // 8
            for pc in range(npieces):
                k0 = pc * 8
                nk = min(8, KC - k0)
                wb = ring[ring_pos[0] % len(ring)]
                ring_pos[0] += 1
                src = W[k0 * 128:(k0 + nk) * 128, c0:c0 + ncols].rearrange("(kc p) n -> p kc n", p=128)
                P.load("pool", wb, wb.ap[:, 0:nk, 0:ncols], src)
                for kk in range(nk):
                    kc = k0 + kk
                    st, sp_ = (kc == 0), (kc == KC - 1)
                    if isinstance(actb, list):
                        ab_, kl = None, 0
                        for (cb_, c0_, c1_) in actb:
                            if c0_ <= kc < c1_:
                                ab_, kl = cb_, kc - c0_
                    else:
                        ab_, kl = actb, kc
                    for u in range(nun):
                        if orient == "F":
                            P.matmul(bl[u], bl[u].ap[:, 0:512], wb, wb.ap[:, kk, u * 128:(u + 1) * 128],
                                     ab_, ab_.ap[:, kl, :], st, sp_)
                        else:
                            P.matmul(bl[u], bl[u].ap[:, 0:ncols], ab_, ab_.ap[:, kl, u * 128:(u + 1) * 128],
                                     wb, wb.ap[:, kk, 0:ncols], st, sp_)
        return out_banks

    def stage_inproj():
        ring = [A.alloc("w", [128, 8, 512], BF16) for _ in range(6)]
        rp = [0]
        actT = [A.alloc("xT", [128, 32, 512], BF16) for _ in range(2)]
        cosbs = [A.alloc("cos", [128, 512], F32) for _ in range(2)]
        sinbs = [A.alloc("sin", [128, 512], F32) for _ in range(2)]
        gng = A.alloc("gng", [128, D], F32)
        P.load("sp", gng, gng.ap, gng_d)
        stb = [A.alloc("stb", [128, 4, 512], BF16) for _ in range(2)]
        stf = [A.alloc("stf", [128, 4, 512], F32) for _ in range(2)]
        tmp = [A.alloc("tmp", [128, 512], F32) for _ in range(4)]
        srr = [0, 0]

        def jobs_for(te0):
            own = te0 >= 2048
            jl = []
            if own:
                for i in range(6):
                    jl.append(("QA", OFF_QA + i * 512, i * 512))
            for i in range(6):
                if own or i >= 4 or te0 >= 1536:
                    jl.append(("KA", OFF_KA + i * 512, i * 512))
            for i in range(6):
                if own or i >= 4 or te0 >= 1536:
                    jl.append(("VA", OFF_VA + i * 512, i * 512))
            if own:
                for i in range(4):
                    jl.append(("QR", OFF_QR + i * 512, i * 512))
            for i in range(4):
                jl.append(("KR", OFF_KR + i * 512, i * 512))
            for i in range(8):
                jl.append(("VR", OFF_VR + i * 512, i * 512))
            if own:
                for i in range(8):
                    jl.append(("GR", OFF_GR + i * 512, i * 512))
                for i in range(8):
                    jl.append(("GA", OFF_GA + i * 512, i * 512))
                for i in range(8):
                    jl.append(("GT", OFF_GT + i * 512, i * 512))
            return jl

        def ld_tile(ti):
            te0 = ti * 512
            ab = actT[ti % 2]
            P.load("pool", ab, ab.ap, xT[:, te0:te0 + 512].rearrange("(kc p) t -> p kc t", p=128))
            P.load("sp", cosbs[ti % 2], cosbs[ti % 2].ap, cos_d[:, te0:te0 + 512])
            P.load("sp", sinbs[ti % 2], sinbs[ti % 2].ap, sin_d[:, te0:te0 + 512])

        ld_tile(0)
        for ti, te0 in enumerate(range(0, TE, 512)):
            own = te0 >= 2048
            to0 = te0 - 2048
            ab = actT[ti % 2]
            cosb, sinb = cosbs[ti % 2], sinbs[ti % 2]
            if ti + 1 < TE // 512:
                ld_tile(ti + 1)
            for (kind, wc0, lc0) in jobs_for(te0):
                orient = "T" if kind in ("VA", "VR", "GR") else "F"
                bl = gemm_job(ring, rp, [dict(W=w_in, c0=wc0, ncols=512, KC=32, act=ab, orient=orient)])[0]
                if kind in ("QA", "KA", "VA", "VR"):
                    st = stb[srr[0] % 2]
                    srr[0] += 1
                    for u in range(4):
                        P.copy(evac_eng(), st.ap[:, u, :], bl[u].ap, [bl[u]], [st])
                    if kind == "QA":
                        dst = QaT[lc0:lc0 + 512, to0:to0 + 512].rearrange("(s p) t -> p s t", p=128)
                    elif kind == "KA":
                        dst = KaT[lc0:lc0 + 512, te0:te0 + 512].rearrange("(s p) t -> p s t", p=128)
                    elif kind == "VA":
                        dst = Va[te0:te0 + 512, lc0:lc0 + 512].rearrange("(s p) c -> p s c", p=128)
                    else:
                        dst = Vr[te0:te0 + 512, lc0:lc0 + 512].rearrange("(s p) c -> p s c", p=128)
                    P.store("sp", st, st.ap, dst)
                elif kind in ("QR", "KR"):
                    st = stb[srr[0] % 2]
                    srr[0] += 1
                    for hp in range(2):
                        b1, b2 = bl[2 * hp], bl[2 * hp + 1]
                        t0, t1, t2, t3 = tmp
                        P.tt(t0.ap, b1.ap, cosb.ap, ALU.mult, [b1, cosb], [t0])
                        P.tt(t1.ap, b2.ap, sinb.ap, ALU.mult, [b2, sinb], [t1])
                        P.tt(st.ap[:, 2 * hp, :], t0.ap, t1.ap, ALU.subtract, [t0, t1], [st])
                        P.tt(t2.ap, b1.ap, sinb.ap, ALU.mult, [b1, sinb], [t2])
                        P.tt(t3.ap, b2.ap, cosb.ap, ALU.mult, [b2, cosb], [t3])
                        P.tt(st.ap[:, 2 * hp + 1, :], t2.ap, t3.ap, ALU.add, [t2, t3], [st])
                    if kind == "QR":
                        dst = QrT[lc0:lc0 + 512, to0:to0 + 512].rearrange("(s p) t -> p s t", p=128)
                    else:
                        dst = KrT[lc0:lc0 + 512, te0:te0 + 512].rearrange("(s p) t -> p s t", p=128)
                    P.store("sp", st, st.ap, dst)
                elif kind == "GR":
                    st = stf[srr[1] % 2]
                    srr[1] += 1
                    for u in range(4):
                        t = tmp[u % 2]
                        P.act(t.ap, bl[u].ap, AF.Silu, [bl[u]], [t])
                        P.tt(st.ap[:, u, :], t.ap, gng.ap[:, lc0:lc0 + 512], ALU.mult, [t, gng], [st])
                    dst = GG[to0:to0 + 512, lc0:lc0 + 512].rearrange("(s p) c -> p s c", p=128)
                    P.store("sp", st, st.ap, dst)
                else:
                    st = stf[srr[1] % 2]
                    srr[1] += 1
                    for u in range(4):
                        P.act(st.ap[:, u, :], bl[u].ap, AF.Sigmoid, [bl[u]], [st])
                    tgt = SigA if kind == "GA" else SigR
                    dst = tgt[lc0:lc0 + 512, to0:to0 + 512].rearrange("(s p) t -> p s t", p=128)
                    P.store("sp", st, st.ap, dst)

    def stage_attn():
        scale = float(128.0 ** -0.5)
        UD = [A.alloc("UD", [128, 2, S_OWN], F32) for _ in range(3)]
        KT = [A.alloc("KT", [128, TE], BF16) for _ in range(2)]
        QT = [A.alloc("QT", [128, S_OWN], BF16) for _ in range(2)]
        VV = [A.alloc("VV", [128, 32 * 128], BF16) for _ in range(2)]
        PT = [A.alloc("PT", [128, 256], BF16) for _ in range(8)]
        ob = [A.alloc("ob", [128, S_OWN], BF16) for _ in range(2)]
        rec = A.alloc("rec", [128, S_OWN], F32)
        LOOK = 3
        cnt = dict(li=0, pt=0, sb=0, ub=0)

        def issue_loads(h, g):
            d = (1, 4, 16)[g]
            halo = 128 * d
            te_lo = 2048 - halo
            ntok = 2048 + halo
            nblk = 16 // d
            row0 = g * 1024 + h * 128
            kt, qt, vv = KT[cnt["li"] % 2], QT[cnt["li"] % 2], VV[cnt["li"] % 2]
            cnt["li"] += 1
            P.load("sp", kt, kt.ap[:, 0:ntok], KaT[row0:row0 + 128, te_lo:TE])
            P.load("sp", qt, qt.ap, QaT[row0:row0 + 128, :])
            vv4 = vv.ap[:, 0:(nblk + 1) * d * 128].rearrange("p (n r e) -> p n r e", r=d, e=128)
            for n in range(nblk + 1):
                src = Va[te_lo + n * 128 * d: te_lo + (n + 1) * 128 * d, row0:row0 + 128].rearrange(
                    "(j r) e -> j r e", r=d)
                P.load("sp", vv, vv4[:, n, :, :], src)
            return (kt, qt, vv, vv4, d, nblk)

        hg = [(h, g) for h in range(8) for g in range(3)]
        loaded = {0: issue_loads(*hg[0])}
        for idx, (h, g) in enumerate(hg):
            if idx + 1 < len(hg):
                loaded[idx + 1] = issue_loads(*hg[idx + 1])
            kt, qt, vv, vv4, d, nblk = loaded.pop(idx)
            udg = UD[g]
            ud4 = udg.ap
            tasks = [(r, n) for r in range(d) for n in range(nblk + 1)]
            pts = {}

            def scores(i):
                r, n = tasks[i]
                kcols = kt.ap[:, n * 128 * d + r:(n + 1) * 128 * d:d]
                if n == 0:
                    q_lo, nq, mk, mkb = 0, 128, maskfirst.ap, maskfirst
                elif n < nblk:
                    q_lo, nq, mk, mkb = (n - 1) * 128, 256, maskcp.ap, maskcp
                else:
                    q_lo, nq, mk, mkb = (n - 1) * 128, 128, maskcp.ap[:, 0:128], maskcp
                qcols = qt.ap[:, q_lo * d + r:(q_lo + nq - 1) * d + r + 1:d]
                sb_ = banks[cnt["sb"] % 4]
                cnt["sb"] += 1
                P.matmul(sb_, sb_.ap[:, 0:nq], kt, kcols, qt, qcols, True, False)
                P.matmul(sb_, sb_.ap[:, 0:nq], ident, ident.ap, mkb, mk, False, True)
                pt = PT[cnt["pt"] % 8]
                cnt["pt"] += 1
                P.act(pt.ap[:, 0:nq], sb_.ap[:, 0:nq], AF.Exp, [sb_], [pt], scale=scale)
                pts[i] = (pt, 0 if n == 0 else 128)

            def pv(i):
                r, n = tasks[i]
                if n == 0:
                    return
                m = n
                ppt, poff = pts[i - 1]
                pt, _ = pts[i]
                ub = banks[4 + cnt["ub"] % 4]
                cnt["ub"] += 1
                P.matmul(ub, ub.ap[:, 0:128], vv, vv4[:, m - 1, r, :], ppt, ppt.ap[:, poff:poff + 128], True, False)
                P.matmul(ub, ub.ap[:, 0:128], vv, vv4[:, m, r, :], pt, pt.ap[:, 0:128], False, True)
                P.matmul(ub, ub.ap[:, 128:256], ones, ones.ap, ppt, ppt.ap[:, poff:poff + 128], True, False)
                P.matmul(ub, ub.ap[:, 128:256], ones, ones.ap, pt, pt.ap[:, 0:128], False, True)
                c_lo = (m - 1) * 128 * d + r
                dst = ud4[:, :, c_lo:c_lo + 127 * d + 1:d]
                src = ub.ap[:, 0:256].rearrange("p (a b) -> p a b", b=128)
                P.copy(evac_eng(), dst, src, [ub], [udg])
                pts.pop(i - 1, None)

            for i in range(len(tasks) + LOOK):
                if i < len(tasks):
                    scores(i)
                if i - LOOK >= 0:
                    pv(i - LOOK)
            if g == 2:
                P.tt(UD[0].ap, UD[0].ap, UD[1].ap, ALU.add, [UD[0], UD[1]], [UD[0]])
                P.tt(UD[0].ap, UD[0].ap, UD[2].ap, ALU.add, [UD[0], UD[2]], [UD[0]])
                P.add("dve", lambda e: e.reciprocal(rec.ap, UD[0].ap[:, 1, :]), reads=[UD[0]], writes=[rec])
                o = ob[h % 2]
                P.tt(o.ap, UD[0].ap[:, 0, :], rec.ap, ALU.mult, [UD[0], rec], [o])
                P.store("pool", o, o.ap, OaT[h * 128:(h + 1) * 128, :])

    def stage_ret():
        intra = A.alloc("intra", [128, 8, 128], F32)
        cdb = A.alloc("cdb", [128, 8, 128], F32)
        P.load("sp", intra, intra.ap, intra_d.rearrange("p (h n) -> p h n", n=128))
        P.load("sp", cdb, cdb.ap, cdb_d.rearrange("p (h n) -> p h n", n=128))
        KT = A.alloc("rKT", [128, 2, TE], BF16)
        QT = A.alloc("rQT", [128, 2, S_OWN], BF16)
        Vq = [A.alloc("rV", [128, 8, 512], BF16) for _ in range(4)]
        Gq = [A.alloc("rG", [128, 8, 512], F32) for _ in range(2)]
        zt = A.alloc("zT", [128, 4, S_OWN], BF16)
        Sf = A.alloc("Sf", [128, 2, 512], F32)
        Sb = A.alloc("Sb", [128, 2, 512], BF16)
        NB = 3
        Ksd = [A.alloc("Ksd", [128, 256], BF16) for _ in range(NB)]
        attb = [A.alloc("att", [128, 128], BF16) for _ in range(NB)]
        Qs = [A.alloc("Qs", [128, 2, 128], BF16) for _ in range(NB)]
        rn = [A.alloc("rn", [128, 512], F32) for _ in range(2)]
        zb = [A.alloc("zb", [128, 512], BF16) for _ in range(2)]
        stats = [A.alloc("stats", [128, 6], F32) for _ in range(2)]
        mv = [A.alloc("mv", [128, 2], F32) for _ in range(2)]
        sd = [A.alloc("sd", [128, 1], F32) for _ in range(2)]
        rstd = [A.alloc("rstd", [128, 1], F32) for _ in range(2)]
        nmr = [A.alloc("nmr", [128, 1], F32) for _ in range(2)]
        bK = [banks[0], banks[5]]
        bS0, bS1 = banks[1], banks[2]
        bA = [banks[3], banks[6]]
        bO = [banks[4], banks[7]]
        bZ = banks[5]
        for h in range(8):
            P.load("sp", KT, KT.ap, KrT[h * 256:(h + 1) * 256, :].rearrange("(dt p) t -> p dt t", p=128))
            P.load("sp", QT, QT.ap, QrT[h * 256:(h + 1) * 256, :].rearrange("(dt p) t -> p dt t", p=128))
            for c4 in range(4):
                P.load("sp", Vq[c4], Vq[c4].ap,
                       Vr[c4 * 1024:(c4 + 1) * 1024, h * 512:(h + 1) * 512].rearrange("(c p) e -> p c e", p=128))
            for c4 in range(2):
                P.load("sp", Gq[c4], Gq[c4].ap,
                       GG[c4 * 1024:(c4 + 1) * 1024, h * 512:(h + 1) * 512].rearrange("(c p) e -> p c e", p=128))
            P.add("dve", lambda e: e.memset(Sf.ap, 0.0), writes=[Sf])
            P.add("dve", lambda e: e.memset(Sb.ap, 0.0), writes=[Sb])
            pending = None

            def flush(pend):
                zb_, co_ = pend
                bzv = bZ.ap.bitcast(BF16)
                for et in range(4):
                    P.transpose(bZ, bzv[:, et * 128:(et + 1) * 128], zb_, zb_.ap[:, et * 128:(et + 1) * 128],
                                ident, ident.ap)
                P.copy(evac_eng(), zt.ap[:, :, co_ * 128:(co_ + 1) * 128],
                       bzv[:, 0:512].rearrange("p (a b) -> p a b", b=128), [bZ], [zt])

            def pre(c):
                own = c >= 16
                co = c - 16
                i3 = c % NB
                bk = bK[0]
                bkv = bk.ap.bitcast(BF16)
                for dt in range(2):
                    P.transpose(bk, bkv[:, dt * 128:(dt + 1) * 128], KT, KT.ap[:, dt, c * 128:(c + 1) * 128],
                                ident, ident.ap)
                ks = Ksd[i3]
                P.act(ks.ap, bkv[:, 0:256], AF.Identity, [bk, sdk], [ks], scale=sdk.ap[:, h:h + 1])
                if own:
                    ba = bA[c % 2]
                    for dt in range(2):
                        P.matmul(ba, ba.ap[:, 0:128], KT, KT.ap[:, dt, c * 128:(c + 1) * 128],
                                 QT, QT.ap[:, dt, co * 128:(co + 1) * 128], dt == 0, dt == 1)
                    at = attb[i3]
                    P.tt(at.ap, ba.ap[:, 0:128], intra.ap[:, h, :], ALU.mult, [ba, intra], [at])
                    qs = Qs[i3]
                    for dt in range(2):
                        P.tt(qs.ap[:, dt, :], QT.ap[:, dt, co * 128:(co + 1) * 128], cdb.ap[:, h, :], ALU.mult,
                             [QT, cdb], [qs])

            pre(0)
            for c in range(32):
                own = c >= 16
                co = c - 16
                i2 = c % 2
                i3 = c % NB
                if c + 1 < 32:
                    pre(c + 1)
                V = Vq[c // 8]
                vc = V.ap[:, c % 8, :]
                ks = Ksd[i3]
                if own:
                    bo = bO[i2]
                    at, qs = attb[i3], Qs[i3]
                    P.matmul(bo, bo.ap, at, at.ap, V, vc, True, False)
                    for dt in range(2):
                        P.matmul(bo, bo.ap, qs, qs.ap[:, dt, :], Sb, Sb.ap[:, dt, :], False, dt == 1)
                for dt, bs in enumerate((bS0, bS1)):
                    P.matmul(bs, bs.ap, ks, ks.ap[:, dt * 128:(dt + 1) * 128], V, vc, True, True)
                if pending is not None:
                    flush(pending)
                    pending = None
                if c < 31:
                    for dt, bs in enumerate((bS0, bS1)):
                        P.stt(Sf.ap[:, dt, :], Sf.ap[:, dt, :], cdk.ap[:, h:h + 1], bs.ap, ALU.mult, ALU.add,
                              [Sf, cdk, bs], [Sf])
                    P.copy("act", Sb.ap, Sf.ap, [Sf], [Sb])
                if own:
                    st_, mv_, sd_, rs_, nm_ = stats[i2], mv[i2], sd[i2], rstd[i2], nmr[i2]
                    P.add("dve", lambda e, a=st_.ap, b=bo.ap: e.bn_stats(a, b), reads=[bo], writes=[st_])
                    P.add("dve", lambda e, a=mv_.ap, b=st_.ap: e.bn_aggr(a, b), reads=[st_], writes=[mv_])
                    P.act(sd_.ap, mv_.ap[:, 1:2], AF.Sqrt, [mv_, epsg], [sd_], bias=epsg.ap, scale=1.0)
                    P.add("dve", lambda e, a=rs_.ap, b=sd_.ap: e.reciprocal(a, b), reads=[sd_], writes=[rs_])
                    P.stt(nm_.ap, mv_.ap[:, 0:1], -1.0, rs_.ap, ALU.mult, ALU.mult, [mv_, rs_], [nm_])
                    r_ = rn[i2]
                    P.act(r_.ap, bo.ap, AF.Identity, [bo, rs_, nm_], [r_], bias=nm_.ap, scale=rs_.ap)
                    z_ = zb[i2]
                    G = Gq[co // 8]
                    P.tt(z_.ap, r_.ap, G.ap[:, co % 8, :], ALU.mult, [r_, G], [z_])
                    pending = (z_, co)
            if pending is not None:
                flush(pending)
            P.store("pool", zt, zt.ap, ZT[h * 512:(h + 1) * 512, :].rearrange("(et p) t -> p et t", p=128))

    def stage_mix():
        ring = [A.alloc("w", [128, 8, 512], BF16) for _ in range(5)]
        rp = [0]
        oa = [A.alloc("oaT", [128, 8, 512], BF16) for _ in range(2)]
        za = [A.alloc("zaT", [128, 32, 512], BF16) for _ in range(2)]
        sga = [A.alloc("sga", [128, 4, 512], F32) for _ in range(2)]
        sgr = [A.alloc("sgr", [128, 4, 512], F32) for _ in range(2)]
        stb = [A.alloc("stb", [128, 4, 512], BF16) for _ in range(2)]
        tmp = [A.alloc("tmp", [128, 512], F32) for _ in range(4)]
        j = 0
        def ld_tile(ti):
            t0 = ti * 512
            P.load("sp", oa[ti % 2], oa[ti % 2].ap, OaT[:, t0:t0 + 512].rearrange("(kc p) t -> p kc t", p=128))
            P.load("sp", za[ti % 2], za[ti % 2].ap, ZT[:, t0:t0 + 512].rearrange("(kc p) t -> p kc t", p=128))

        ld_tile(0)
        for ti in range(4):
            t0 = ti * 512
            ob_, zb_ = oa[ti % 2], za[ti % 2]
            if ti + 1 < 4:
                ld_tile(ti + 1)
            for cb in range(8):
                c0 = cb * 512
                sa, sr, st = sga[j % 2], sgr[j % 2], stb[j % 2]
                j += 1
                P.load("sp", sa, sa.ap, SigA[c0:c0 + 512, t0:t0 + 512].rearrange("(s p) t -> p s t", p=128))
                P.load("sp", sr, sr.ap, SigR[c0:c0 + 512, t0:t0 + 512].rearrange("(s p) t -> p s t", p=128))
                ba, br = gemm_job(ring, rp, [
                    dict(W=w_ao, c0=c0, ncols=512, KC=8, act=ob_, orient="F"),
                    dict(W=w_ro, c0=c0, ncols=512, KC=32, act=zb_, orient="F")])
                for u in range(4):
                    ta, tr = tmp[2 * (u % 2)], tmp[2 * (u % 2) + 1]
                    P.tt(ta.ap, ba[u].ap, sa.ap[:, u, :], ALU.mult, [ba[u], sa], [ta])
                    P.tt(tr.ap, br[u].ap, sr.ap[:, u, :], ALU.mult, [br[u], sr], [tr])
                    P.tt(st.ap[:, u, :], ta.ap, tr.ap, ALU.add, [ta, tr], [st])
                P.store("sp", st, st.ap, MIXT[c0:c0 + 512, t0:t0 + 512].rearrange("(s p) t -> p s t", p=128))

    def stage_wo():
        ring = [A.alloc("w", [128, 8, 512], BF16) for _ in range(6)]
        rp = [0]
        ma = [A.alloc("mT", [128, 32, 512], BF16) for _ in range(2)]
        xt = [A.alloc("xt", [128, 4, 512], F32) for _ in range(2)]
        stf = [A.alloc("stf", [128, 4, 512], F32) for _ in range(2)]
        j = 0
        def ld_tile(ti):
            P.load("sp", ma[ti % 2], ma[ti % 2].ap,
                   MIXT[:, ti * 512:ti * 512 + 512].rearrange("(kc p) t -> p kc t", p=128))

        ld_tile(0)
        for ti in range(4):
            t0 = ti * 512
            mb = ma[ti % 2]
            if ti + 1 < 4:
                ld_tile(ti + 1)
            for cb in range(8):
                c0 = cb * 512
                xb, st = xt[j % 2], stf[j % 2]
                j += 1
                P.load("sp", xb, xb.ap, x_own[t0:t0 + 512, c0:c0 + 512].rearrange("(s p) c -> p s c", p=128))
                bl = gemm_job(ring, rp, [dict(W=w_o, c0=c0, ncols=512, KC=32, act=mb, orient="T")])[0]
                for u in range(4):
                    P.stt(st.ap[:, u, :], xb.ap[:, u, :], ALPHA, bl[u].ap, ALU.mult, ALU.add, [xb, bl[u]], [st])
                P.store("sp", st, st.ap, PRE1[t0:t0 + 512, c0:c0 + 512].rearrange("(s p) c -> p s c", p=128))

    def stage_ln(src, g_d, b_d, dst_f32, dst_T):
        gt = A.alloc("lng", [128, D], F32)
        bt = A.alloc("lnb", [128, D], F32)
        P.load("sp", gt, gt.ap, g_d)
        P.load("sp", bt, bt.ap, b_d)
        NX = 3
        xr = [A.alloc("xr", [128, D], F32) for _ in range(NX)]
        yr = [A.alloc("yr", [128, D], F32) for _ in range(2)]
        stats = [A.alloc("stats", [128, 8, 6], F32) for _ in range(NX)]
        mv = [A.alloc("mv", [128, 2], F32) for _ in range(NX)]
        sd = [A.alloc("sd", [128, 1], F32) for _ in range(NX)]
        rstd = [A.alloc("rstd", [128, 1], F32) for _ in range(NX)]
        nmr = [A.alloc("nmr", [128, 1], F32) for _ in range(NX)]
        if dst_T is not None:
            hb = [A.alloc("hb", [128, D], BF16) for _ in range(2)]
            hT = [A.alloc("hT", [128, 32, 512], BF16) for _ in range(1)]

        def phA(rt):
            i = rt % NX
            x_, st_, mv_ = xr[i], stats[i], mv[i]
            P.load("sp", x_, x_.ap, src[rt * 128:(rt + 1) * 128, :])
            for k in range(8):
                P.add("dve", lambda e, a=st_.ap[:, k, :], b=x_.ap[:, k * 512:(k + 1) * 512]: e.bn_stats(a, b),
                      reads=[x_], writes=[st_])
            P.add("dve", lambda e, a=mv_.ap, b=st_.ap.rearrange("p a b -> p (a b)"): e.bn_aggr(a, b),
                  reads=[st_], writes=[mv_])

        def phB(rt):
            i = rt % NX
            x_, y_ = xr[i], yr[rt % 2]
            mv_, sd_, rs_, nm_ = mv[i], sd[i], rstd[i], nmr[i]
            P.act(sd_.ap, mv_.ap[:, 1:2], AF.Sqrt, [mv_, epsl], [sd_], bias=epsl.ap, scale=1.0)
            P.add("dve", lambda e, a=rs_.ap, b=sd_.ap: e.reciprocal(a, b), reads=[sd_], writes=[rs_])
            P.stt(nm_.ap, mv_.ap[:, 0:1], -1.0, rs_.ap, ALU.mult, ALU.mult, [mv_, rs_], [nm_])
            P.act(y_.ap, x_.ap, AF.Identity, [x_, rs_, nm_], [y_], bias=nm_.ap, scale=rs_.ap)

        def phC(rt):
            y_ = yr[rt % 2]
            P.tt(y_.ap, y_.ap, gt.ap, ALU.mult, [y_, gt], [y_])
            P.tt(y_.ap, y_.ap, bt.ap, ALU.add, [y_, bt], [y_])
            P.store("pool", y_, y_.ap, dst_f32[rt * 128:(rt + 1) * 128, :])
            if dst_T is not None:
                hb_ = hb[rt % 2]
                P.copy("act", hb_.ap, y_.ap, [y_], [hb_])
                ht = hT[0]
                sub = rt % 4
                for k4 in range(4):
                    bk = banks[(rt * 4 + k4) % 8]
                    bkv = bk.ap.bitcast(BF16)
                    for kk in range(8):
                        kc = k4 * 8 + kk
                        P.transpose(bk, bkv[:, kk * 128:(kk + 1) * 128], hb_, hb_.ap[:, kc * 128:(kc + 1) * 128],
                                    ident, ident.ap)
                    P.copy(evac_eng(), ht.ap[:, k4 * 8:(k4 + 1) * 8, sub * 128:(sub + 1) * 128],
                           bkv.rearrange("p (a b) -> p a b", b=128), [bk], [ht])
                if sub == 3:
                    t0 = (rt // 4) * 512
                    P.store("pool", ht, ht.ap, dst_T[:, t0:t0 + 512].rearrange("(kc p) t -> p kc t", p=128))

        for t in range(16 + 2):
            if t < 16:
                phA(t)
            if 0 <= t - 1 < 16:
                phB(t - 1)
            if 0 <= t - 2 < 16:
                phC(t - 2)

    def stage_ffn_up():
        ring = [A.alloc("w", [128, 8, 512], BF16) for _ in range(8)]
        rp = [0]
        ha = [A.alloc("hT", [128, 32, 512], BF16) for _ in range(2)]
        stb = [A.alloc("stb", [128, 4, 512], BF16) for _ in range(2)]
        tmp = [A.alloc("tmp", [128, 512], F32) for _ in range(2)]
        j = 0
        def ld_tile(ti):
            P.load("sp", ha[ti % 2], ha[ti % 2].ap,
                   H1T[:, ti * 512:ti * 512 + 512].rearrange("(kc p) t -> p kc t", p=128))

        ld_tile(0)
        for ti in range(4):
            t0 = ti * 512
            hb_ = ha[ti % 2]
            if ti + 1 < 4:
                ld_tile(ti + 1)
            for c0 in range(0, FFN, 512):
                ncols = min(512, FFN - c0)
                nsub = ncols // 128
                st = stb[j % 2]
                j += 1
                bg, bu = gemm_job(ring, rp, [
                    dict(W=w_fg, c0=c0, ncols=ncols, KC=32, act=hb_, orient="F"),
                    dict(W=w_fu, c0=c0, ncols=ncols, KC=32, act=hb_, orient="F")])
                for u in range(nsub):
                    t = tmp[u % 2]
                    P.act(t.ap, bg[u].ap, AF.Silu, [bg[u]], [t])
                    P.tt(st.ap[:, u, :], t.ap, bu[u].ap, ALU.mult, [t, bu[u]], [st])
                P.store("sp", st, st.ap[:, 0:nsub, :],
                        HIDT[c0:c0 + ncols, t0:t0 + 512].rearrange("(s p) t -> p s t", p=128))

    def stage_ple():
        ring = [A.alloc("w", [128, 8, 512], BF16) for _ in range(6)]
        rp = [0]
        ha = [A.alloc("hT", [128, 32, 512], BF16) for _ in range(2)]
        pa = [A.alloc("pT", [128, 2, 512], BF16) for _ in range(2)]
        h1t = [A.alloc("h1t", [128, 4, 512], F32) for _ in range(2)]
        stf = [A.alloc("stf", [128, 4, 512], F32) for _ in range(2)]
        tmp = [A.alloc("tmp", [128, 512], F32) for _ in range(4)]
        j = 0
        def ld_tile(ti):
            t0 = ti * 512
            P.load("sp", ha[ti % 2], ha[ti % 2].ap, H1T[:, t0:t0 + 512].rearrange("(kc p) t -> p kc t", p=128))
            P.load("pool", pa[ti % 2], pa[ti % 2].ap, pT_d[:, t0:t0 + 512].rearrange("(kc p) t -> p kc t", p=128))

        ld_tile(0)
        for ti in range(4):
            t0 = ti * 512
            hb_, pb_ = ha[ti % 2], pa[ti % 2]
            if ti + 1 < 4:
                ld_tile(ti + 1)
            for cb in range(8):
                c0 = cb * 512
                hx, st = h1t[j % 2], stf[j % 2]
                j += 1
                P.load("sp", hx, hx.ap, H1[t0:t0 + 512, c0:c0 + 512].rearrange("(s p) c -> p s c", p=128))
                bg, bu = gemm_job(ring, rp, [
                    dict(W=w_pg, c0=c0, ncols=512, KC=32, act=hb_, orient="T"),
                    dict(W=w_pu, c0=c0, ncols=512, KC=2, act=pb_, orient="T")])
                for u in range(4):
                    ta, tb = tmp[2 * (u % 2)], tmp[2 * (u % 2) + 1]
                    P.act(ta.ap, bg[u].ap, AF.Sigmoid, [bg[u]], [ta])
                    P.tt(tb.ap, ta.ap, bu[u].ap, ALU.mult, [ta, bu[u]], [tb])
                    P.stt(st.ap[:, u, :], hx.ap[:, u, :], ALPHA, tb.ap, ALU.mult, ALU.add, [hx, tb], [st])
                P.store("sp", st, st.ap, PRE2[t0:t0 + 512, c0:c0 + 512].rearrange("(s p) c -> p s c", p=128))

    def stage_ffn_down():
        KC = FFN // 128
        ring = [A.alloc("w", [128, 8, 512], BF16) for _ in range(6)]
        rp = [0]
        hch = []
        for k0 in range(0, KC, 16):
            k1 = min(KC, k0 + 16)
            hch.append((A.alloc("hidT", [128, k1 - k0, 512], BF16), k0, k1))
        prt = [A.alloc("prt", [128, 4, 512], F32) for _ in range(2)]
        stf = [A.alloc("stf", [128, 4, 512], F32) for _ in range(2)]
        j = 0

        def ld_hid(ti):
            for (cb_, k0, k1) in hch:
                P.load("sp", cb_, cb_.ap,
                       HIDT[k0 * 128:k1 * 128, ti * 512:ti * 512 + 512].rearrange("(kc p) t -> p kc t", p=128))

        for ti in range(4):
            t0 = ti * 512
            hb_ = hch
            if ti == 0:
                ld_hid(0)
            for cb in range(8):
                c0 = cb * 512
                px, st = prt[j % 2], stf[j % 2]
                j += 1
                P.load("sp", px, px.ap, PRE2[t0:t0 + 512, c0:c0 + 512].rearrange("(s p) c -> p s c", p=128))
                if cb == 7 and ti + 1 < 4:
                    pend_ld = ti + 1
                else:
                    pend_ld = None
                bl = gemm_job(ring, rp, [dict(W=w_fd, c0=c0, ncols=512, KC=KC, act=hb_, orient="T")])[0]
                if pend_ld is not None:
                    ld_hid(pend_ld)
                for u in range(4):
                    P.tt(st.ap[:, u, :], px.ap[:, u, :], bl[u].ap, ALU.add, [px, bl[u]], [st])
                P.store("sp", st, st.ap, PRE3[t0:t0 + 512, c0:c0 + 512].rearrange("(s p) c -> p s c", p=128))

    stages = [
        ("inproj", stage_inproj),
        ("attn", stage_attn),
        ("ret", stage_ret),
        ("mix", stage_mix),
        ("wo", stage_wo),
        ("ln1", lambda: stage_ln(PRE1, ln1g_d, ln1b_d, H1, H1T)),
        ("ffn_up", stage_ffn_up),
        ("ple", stage_ple),
        ("ffn_down", stage_ffn_down),
        ("ln2", lambda: stage_ln(PRE3, ln2g_d, ln2b_d, out_d, None)),
    ]
    for i, (name, fn) in enumerate(stages):
        if i >= upto:
            break
        P.barrier()
        A.reset()
        fn()
    P.barrier()
    P.emit(nc, es)
    es.close()
    return nc


def _const_tables(hf):
    bf = ml_dtypes.bfloat16
    t = {}
    t["ident"] = np.eye(128, dtype=np.float32).astype(bf)
    t["ones"] = np.ones((128, 128), np.float32).astype(bf)
    k = np.arange(128)[:, None]
    q = np.arange(128)[None, :]
    cur = np.where(q >= k, 0.0, NEG).astype(np.float32)
    prev = np.where(k >= q, 0.0, NEG).astype(np.float32)
    t["maskcp"] = np.concatenate([cur, prev], axis=1).astype(bf)
    t["maskfirst"] = (prev if hf == 1 else np.full((128, 128), NEG, np.float32)).astype(bf)
    hh = np.arange(8, dtype=np.float64)
    lg = np.log(1.0 - 2.0 ** (-5.0 - hh))
    m = np.arange(128)[:, None].astype(np.float64)
    n = np.arange(128)[None, :].astype(np.float64)
    intra = np.zeros((128, 8, 128), np.float64)
    cdbt = np.zeros((128, 8, 128), np.float64)
    for h in range(8):
        intra[:, h, :] = np.where(n >= m, np.exp(np.maximum(n - m, 0.0) * lg[h]), 0.0) / 16.0
        cdbt[:, h, :] = np.exp((n + 1.0) * lg[h])
    t["intraT"] = intra.reshape(128, 1024).astype(np.float32)
    t["cdb"] = cdbt.reshape(128, 1024).astype(np.float32)
    t["sdk"] = (np.exp((127.0 - m) * lg[None, :]) / 16.0).astype(np.float32)
    t["cdk"] = np.broadcast_to(np.exp(128.0 * lg)[None, :], (128, 8)).astype(np.float32).copy()
    half = 128
    inv_freq = (np.float32(10000.0) ** (-(np.arange(half, dtype=np.float32)) / np.float32(half))).astype(np.float32)
    pos = (hf * 2048 - 2048 + np.arange(TE)).astype(np.float32)
    ang = (inv_freq[:, None] * pos[None, :]).astype(np.float32)
    t["cosT"] = np.cos(ang.astype(np.float64)).astype(np.float32)
    t["sinT"] = np.sin(ang.astype(np.float64)).astype(np.float32)
    return t


def make_in_maps(x, p, w_in, w_attn_out, w_ret_out, ret_gn_g, w_o, ln1_g, ln1_b, w_ffn_gate, w_ffn_up,
                 w_ffn_down, w_ple_gate, w_ple_up, ln2_g, ln2_b):
    f = lambda a: np.ascontiguousarray(np.asarray(a, dtype=np.float32))
    x = f(x)
    p = f(p)
    shared = {
        "w_in": f(w_in)[0], "w_attn_out": f(w_attn_out)[0], "w_ret_out": f(w_ret_out)[0], "w_o": f(w_o)[0],
        "w_ffn_gate": f(w_ffn_gate)[0], "w_ffn_up": f(w_ffn_up)[0], "w_ffn_down": f(w_ffn_down)[0],
        "w_ple_gate": f(w_ple_gate)[0], "w_ple_up": f(w_ple_up)[0],
        "gng": np.ascontiguousarray(np.broadcast_to(f(ret_gn_g)[0][None, :], (128, D))),
        "ln1g": np.ascontiguousarray(np.broadcast_to(f(ln1_g)[0][None, :], (128, D))),
        "ln1b": np.ascontiguousarray(np.broadcast_to(f(ln1_b)[0][None, :], (128, D))),
        "ln2g": np.ascontiguousarray(np.broadcast_to(f(ln2_g)[0][None, :], (128, D))),
        "ln2b": np.ascontiguousarray(np.broadcast_to(f(ln2_b)[0][None, :], (128, D))),
    }
    tabs = [_const_tables(0), _const_tables(1)]
    in_maps = []
    for c in range(8):
        b, hf = c // 2, c % 2
        own = x[b, hf * 2048:(hf + 1) * 2048]
        if hf == 1:
            ext = x[b]
        else:
            ext = np.concatenate([np.zeros_like(own), own], axis=0)
        m = dict(shared)
        m["xT"] = np.ascontiguousarray(ext.T)
        m["x_own"] = np.ascontiguousarray(own)
        m["pT"] = np.ascontiguousarray(p[0, b, hf * 2048:(hf + 1) * 2048].T)
        m.update(tabs[hf])
        in_maps.append(m)
    return in_maps


_NC_CACHE = {}


def kernel(**inputs):
    in_maps = make_in_maps(**inputs)
    if "nc" not in _NC_CACHE:
        _NC_CACHE["nc"] = build_program()
    nc = _NC_CACHE["nc"]
    res = run_bass_kernel_spmd(nc, in_maps, core_ids=list(range(8)))
    out = np.empty((4, 4096, D), np.float32)
    for c in range(8):
        b, hf = c // 2, c % 2
        out[b, hf * 2048:(hf + 1) * 2048] = res.results[c]["out"]
    return out
```
